# Optimizing a Trainium2 kernel written in Bass

```python
import jax, jax.numpy as jnp
from jax import lax
import numpy as np

D_MODEL = 1024
BATCH = 1
SEQ = 16384
DEPTH = 1

N_META = 16
BLOCK = 128
MIX_WIDTH = D_MODEL
HG_HEADS = 4
HG_DK = 128
HG_DV = (MIX_WIDTH // 2) // HG_HEADS
SB_HEADS = 8
SB_DH = (MIX_WIDTH // 2) // SB_HEADS
HG_KEY = HG_HEADS * HG_DK
HG_VAL = HG_HEADS * HG_DV
SB_W = SB_HEADS * SB_DH
IN_COLS = 2 * HG_KEY + 2 * HG_VAL + 3 * SB_W
D_FF = ((8 * D_MODEL // 3 + 127) // 128) * 128
RMS_EPS = 1e-6

kernel_name = "hymba_hgrn2_stickbreaking_macaron"


def rms_norm(x, gain):
    xf = x.astype(jnp.float32)
    y = xf * lax.rsqrt(jnp.mean(xf * xf, axis=-1, keepdims=True) + RMS_EPS)
    return (y * gain.astype(jnp.float32)).astype(x.dtype)


def swiglu(h, w_in, w_out):
    gu = h @ w_in
    g, u = jnp.split(gu, 2, axis=-1)
    return (jax.nn.silu(g) * u) @ w_out


def split_heads(t, n_heads):
    b, l, w = t.shape
    return t.reshape(b, l, n_heads, w // n_heads).transpose(0, 2, 1, 3)


def hgrn2(q_raw, f_raw, i_raw, g_raw, lb, out_gain, valid):
    b, L, _ = q_raw.shape
    n_chunks = L // BLOCK
    lbf = lb.astype(jnp.float32)
    z = f_raw.astype(jnp.float32)
    f = lbf + (1.0 - lbf) * jax.nn.sigmoid(z)
    vmask = valid[None, :, None]
    logf = jnp.where(vmask, jnp.log(f), 0.0)
    k = jnp.where(vmask, (1.0 - lbf) * jax.nn.sigmoid(-z), 0.0)
    q = jax.nn.silu(q_raw.astype(jnp.float32))
    v = i_raw.astype(jnp.float32)

    def to_chunks(t, n_heads):
        t = split_heads(t, n_heads)
        t = t.reshape(b, n_heads, n_chunks, BLOCK, t.shape[-1])
        return jnp.moveaxis(t, 2, 0)

    qc, kc, vc, lfc = (to_chunks(q, HG_HEADS), to_chunks(k, HG_HEADS),
                       to_chunks(v, HG_HEADS), to_chunks(logf, HG_HEADS))
    causal = jnp.tril(jnp.ones((BLOCK, BLOCK), dtype=bool))[:, :, None]

    def step(S, inp):
        qk, kk, vk, lf = inp
        bcum = jnp.cumsum(lf, axis=-2)
        o_inter = jnp.einsum('bhtd,bhdv->bhtv', qk * jnp.exp(bcum), S)
        diff = bcum[:, :, :, None, :] - bcum[:, :, None, :, :]
        decay = jnp.exp(jnp.where(causal, diff, -jnp.inf))
        attn = jnp.einsum('bhtd,bhsd,bhtsd->bhts', qk, kk, decay)
        o_intra = jnp.einsum('bhts,bhsv->bhtv', attn, vk)
        b_last = bcum[:, :, -1:, :]
        S_new = (jnp.exp(b_last[:, :, 0, :])[..., None] * S
                 + jnp.einsum('bhsd,bhsv->bhdv', kk * jnp.exp(b_last - bcum), vk))
        return S_new, o_inter + o_intra

    S0 = jnp.zeros((b, HG_HEADS, HG_DK, HG_DV), jnp.float32)
    _, ys = lax.scan(step, S0, (qc, kc, vc, lfc))
    o = jnp.moveaxis(ys, 0, 2).reshape(b, HG_HEADS, L, HG_DV).transpose(0, 2, 1, 3)
    o = rms_norm(o, out_gain.reshape(HG_HEADS, HG_DV)).reshape(b, L, HG_VAL)
    return o * jax.nn.silu(g_raw.astype(jnp.float32))


def stick_breaking(q_raw, k_raw, v_raw, q_gain, k_gain, valid):
    b, L, _ = q_raw.shape
    n_blocks = L // BLOCK
    q = rms_norm(split_heads(q_raw.astype(jnp.float32), SB_HEADS), q_gain)
    k = rms_norm(split_heads(k_raw.astype(jnp.float32), SB_HEADS), k_gain)
    v = split_heads(v_raw.astype(jnp.float32), SB_HEADS)
    scale = 1.0 / np.sqrt(SB_DH).astype(np.float32)
    kpos = jnp.arange(L, dtype=jnp.int32)
    qb = jnp.moveaxis(q.reshape(b, SB_HEADS, n_blocks, BLOCK, SB_DH), 2, 0)
    starts = jnp.arange(n_blocks, dtype=jnp.int32) * BLOCK

    def block(args):
        qblk, start = args
        qpos = start + jnp.arange(BLOCK, dtype=jnp.int32)
        z = jnp.einsum('bhqd,bhkd->bhqk', qblk, k) * scale
        allowed = (kpos[None, :] < qpos[:, None]) & valid[None, :]
        log_beta = jnp.where(allowed, jax.nn.log_sigmoid(z), -jnp.inf)
        log_keep = jnp.where(allowed, jax.nn.log_sigmoid(-z), 0.0)
        suffix = lax.cumsum(log_keep, axis=3, reverse=True) - log_keep
        w = jnp.exp(log_beta + suffix)
        return jnp.einsum('bhqk,bhkd->bhqd', w, v)

    out = lax.map(block, (qb, starts))
    out = jnp.moveaxis(out, 0, 2).reshape(b, SB_HEADS, L, SB_DH)
    return out.transpose(0, 2, 1, 3).reshape(b, L, SB_W)


def setup_inputs(seed: int = 0) -> dict:
    key = jax.random.key(seed)
    ks = jax.random.split(key, 16)
    f32 = jnp.float32
    nrm = lambda k, shape, s: jax.random.normal(k, shape, f32) * s
    gain = lambda k, shape: 1.0 + 0.02 * jax.random.normal(k, shape, f32)
    return {
        "x": jax.random.normal(ks[0], (BATCH, SEQ, D_MODEL), f32),
        "meta_tokens": nrm(ks[1], (N_META, D_MODEL), 1.0),
        "ffn1_norm": gain(ks[2], (DEPTH, D_MODEL)),
        "ffn1_w_in": nrm(ks[3], (DEPTH, D_MODEL, 2 * D_FF), D_MODEL ** -0.5),
        "ffn1_w_out": nrm(ks[4], (DEPTH, D_FF, D_MODEL), D_FF ** -0.5),
        "mix_norm": gain(ks[5], (DEPTH, D_MODEL)),
        "w_in": nrm(ks[6], (DEPTH, D_MODEL, IN_COLS), D_MODEL ** -0.5),
        "hgrn_lb_logits": nrm(ks[7], (DEPTH + 1, HG_KEY), 0.5),
        "hgrn_out_norm": gain(ks[8], (DEPTH, HG_VAL)),
        "sb_q_norm": gain(ks[9], (DEPTH, SB_DH)),
        "sb_k_norm": gain(ks[10], (DEPTH, SB_DH)),
        "w_out": nrm(ks[11], (DEPTH, MIX_WIDTH, D_MODEL), MIX_WIDTH ** -0.5),
        "ffn2_norm": gain(ks[12], (DEPTH, D_MODEL)),
        "ffn2_w_in": nrm(ks[13], (DEPTH, D_MODEL, 2 * D_FF), D_MODEL ** -0.5),
        "ffn2_w_out": nrm(ks[14], (DEPTH, D_FF, D_MODEL), D_FF ** -0.5),
    }


def reference(x, meta_tokens, ffn1_norm, ffn1_w_in, ffn1_w_out, mix_norm, w_in,
              hgrn_lb_logits, hgrn_out_norm, sb_q_norm, sb_k_norm, w_out,
              ffn2_norm, ffn2_w_in, ffn2_w_out):
    b = x.shape[0]
    pad = (-N_META) % BLOCK
    meta = jnp.broadcast_to(meta_tokens.astype(x.dtype)[None], (b, N_META, D_MODEL))
    h = jnp.concatenate([jnp.zeros((b, pad, D_MODEL), x.dtype), meta, x], axis=1)
    Lp = h.shape[1]
    valid = jnp.arange(Lp, dtype=jnp.int32) >= pad
    lb_all = jnp.cumsum(jax.nn.softmax(hgrn_lb_logits.astype(jnp.float32), axis=0), axis=0)

    for l in range(DEPTH):
        h = h + 0.5 * swiglu(rms_norm(h, ffn1_norm[l]), ffn1_w_in[l], ffn1_w_out[l])
        u = rms_norm(h, mix_norm[l]) @ w_in[l]
        hq, hf, hi, hg, sq, sk, sv = jnp.split(
            u, np.cumsum([HG_KEY, HG_KEY, HG_VAL, HG_VAL, SB_W, SB_W])[:6].tolist(), axis=-1)
        o_hg = hgrn2(hq, hf, hi, hg, lb_all[l], hgrn_out_norm[l], valid)
        o_sb = stick_breaking(sq, sk, sv, sb_q_norm[l], sb_k_norm[l], valid)
        o = jnp.concatenate([o_hg, o_sb], axis=-1).astype(h.dtype)
        h = h + o @ w_out[l]
        h = h + 0.5 * swiglu(rms_norm(h, ffn2_norm[l]), ffn2_w_in[l], ffn2_w_out[l])

    return h[:, pad + N_META:, :]
```

```python
import numpy as np
import ml_dtypes
from contextlib import ExitStack

import concourse.bass as bass
import concourse.mybir as mybir
from concourse.bass_utils import run_bass_kernel_spmd

F32 = mybir.dt.float32
BF16 = mybir.dt.bfloat16
AF = mybir.ActivationFunctionType
ALU = mybir.AluOpType
AX = mybir.AxisListType

NCORES = 8
P = 128
D = 1024
DFF = 2816
NF = DFF // P
NBL = 17
NQ = 16
INC = 3584
EPS = 1e-6
NEG = -30000.0


class _Tok:
    __slots__ = ("sem", "val", "eng")

    def __init__(self, sem, val, eng):
        self.sem, self.val, self.eng = sem, val, eng


class _Buf:
    __slots__ = ("name", "w", "r")

    def __init__(self, name):
        self.name, self.w, self.r = name, None, []


class _Eng:
    def __init__(self, name, h, sem):
        self.name, self.h, self.sem = name, h, sem
        self.cnt = 0
        self.pend = None
        self.waited = {}


class Sched:
    def __init__(self, nc, es):
        self.nc, self.es = nc, es
        mk = lambda n: es.enter_context(nc.semaphore(n))
        self.E = {
            "pe": _Eng("pe", nc.tensor, mk("s_pe")),
            "act": _Eng("act", nc.scalar, mk("s_act")),
            "dve": _Eng("dve", nc.vector, mk("s_dve")),
            "pool": _Eng("pool", nc.gpsimd, mk("s_pool")),
            "sp": _Eng("sp", nc.sync, mk("s_sp")),
        }
        self.dsem = {}
        self.dsem_by_sem = {}
        self.bufs = {}
        self.ccsem = mk("s_cc")
        self.ccn = 0
        self.nops = 0

    def B(self, name):
        if name not in self.bufs:
            self.bufs[name] = _Buf(name)
        return self.bufs[name]

    def _need(self, e, tok, need):
        if tok is None:
            return
        if tok.eng is e and e.name == "pe":
            return
        if tok.val is None:
            raise RuntimeError(f"dependency on unsignalled op of {tok.eng.name} from {e.name}")
        k = id(tok.sem)
        val = tok.val
        if tok.eng is None:
            val = self.dsem_by_sem[k][1]
        if need.get(k, (None, 0))[1] < val:
            need[k] = (tok.sem, val)

    def _collect(self, e, reads, writes, skip_sem=None):
        need = {}
        for b in reads:
            b = self.B(b)
            if b.w is not None and b.w.sem is not skip_sem:
                self._need(e, b.w, need)
        for b in writes:
            b = self.B(b)
            if b.w is not None and b.w.sem is not skip_sem:
                self._need(e, b.w, need)
            for t in b.r:
                self._need(e, t, need)
        for k, (sem, val) in need.items():
            if e.waited.get(k, 0) < val:
                e.h.wait_ge(sem, val)
                e.waited[k] = val

    def _record(self, tok, reads, writes):
        for b in reads:
            b = self.B(b)
            b.r = [t for t in b.r if not (t.sem is tok.sem and t is not tok and t.val is not None
                                          and tok.val is not None and t.val <= tok.val)]
            if tok not in b.r:
                b.r.append(tok)
        for b in writes:
            b = self.B(b)
            b.w = tok
            b.r = []

    def op(self, en, fn, reads=(), writes=(), sig=True):
        e = self.E[en]
        ex = [b for b in reads if b.startswith("ps") or b == "TPb"]
        if ex:
            reads = [b for b in reads if b not in ex]
            writes = list(writes) + ex
        self._collect(e, reads, writes)
        ins = fn()
        self.nops += 1
        if e.pend is None:
            e.pend = _Tok(e.sem, None, e)
        tok = e.pend
        self._record(tok, reads, writes)
        if sig:
            e.cnt += 1
            ins.then_inc(e.sem, 1)
            tok.val = e.cnt
            e.pend = None
        return ins

    def dma(self, qn, out, in_, reads=(), writes=(), key="d"):
        e = self.E[qn]
        if key not in self.dsem:
            self.dsem[key] = [self.es.enter_context(self.nc.semaphore("sd_" + key)), 0]
            self.dsem_by_sem[id(self.dsem[key][0])] = self.dsem[key]
        ds = self.dsem[key]
        self._collect(e, reads, writes, skip_sem=ds[0])
        ins = e.h.dma_start(out=out, in_=in_)
        ds[1] += 16
        ins.then_inc(ds[0], 16)
        tok = _Tok(ds[0], ds[1], None)
        self._record(tok, reads, writes)
        self.nops += 1
        return ins

    def allgather(self, in_ap, out_ap, reads, writes, scratch):
        e = self.E["pool"]
        self._collect(e, reads, writes)
        ins = e.h.collective_compute("AllGather", ALU.bypass, replica_groups=[list(range(NCORES))],
                                     ins=[in_ap.opt()], outs=[out_ap.opt()])
        self.ccn += 1
        ins.then_inc(self.ccsem)
        e.h.wait_ge(self.ccsem, self.ccn)
        self.op("pool", lambda: e.h.memset(scratch, 0.0), reads=reads, writes=list(writes) + ["_ccscratch"])

    def barrier(self):
        toks = []
        for e in self.E.values():
            assert e.pend is None, f"pending unsignalled op on {e.name} at barrier"
            if e.cnt:
                toks.append((e.sem, e.cnt, e))
        for key, (sem, cnt) in self.dsem.items():
            if cnt:
                toks.append((sem, cnt, None))
        for e in self.E.values():
            for sem, val, src in toks:
                if src is e:
                    continue
                k = id(sem)
                if e.waited.get(k, 0) < val:
                    e.h.wait_ge(sem, val)
                    e.waited[k] = val

    def final_wait(self, keys):
        e = self.E["sp"]
        for key in keys:
            sem, cnt = self.dsem[key]
            e.h.wait_ge(sem, cnt)


def build_program(dbg=None):
    dbg = dbg or {}
    nc = bass.Bass("TRN2", target_bir_lowering=False)
    dt = nc.dram_tensor
    x_d = dt("x", [NBL, P, D], F32, kind="ExternalInput").ap()
    xall_d = dt("xall", [129, P, D], F32, kind="ExternalInput").ap()
    f1n_d = dt("ffn1_norm", [1, D], F32, kind="ExternalInput").ap()
    f1wi_d = dt("ffn1_w_in", [D, 2 * DFF], F32, kind="ExternalInput").ap()
    f1wo_d = dt("ffn1_w_out", [DFF, D], F32, kind="ExternalInput").ap()
    mixn_d = dt("mix_norm", [1, D], F32, kind="ExternalInput").ap()
    win_d = dt("w_in", [D, INC], F32, kind="ExternalInput").ap()
    lbl_d = dt("hgrn_lb_logits", [1, 1024], F32, kind="ExternalInput").ap()
    ogn_d = dt("hgrn_out_norm", [1, 512], F32, kind="ExternalInput").ap()
    qg_d = dt("sb_q_norm", [1, 64], F32, kind="ExternalInput").ap()
    kg_d = dt("sb_k_norm", [1, 64], F32, kind="ExternalInput").ap()
    wout_d = dt("w_out", [D, D], F32, kind="ExternalInput").ap()
    f2n_d = dt("ffn2_norm", [1, D], F32, kind="ExternalInput").ap()
    f2wi_d = dt("ffn2_w_in", [D, 2 * DFF], F32, kind="ExternalInput").ap()
    f2wo_d = dt("ffn2_w_out", [DFF, D], F32, kind="ExternalInput").ap()
    cmat_d = dt("cmat", [P, 8 * P], BF16, kind="ExternalInput").ap()
    pmask_d = dt("pmask", [P, 16 * P], BF16, kind="ExternalInput").ap()
    pvec_d = dt("pvec", [P, 16], F32, kind="ExternalInput").ap()
    y_d = dt("y", [NQ, P, D], F32, kind="ExternalOutput").ap()
    dbg_d = {}
    for name, shape in dbg.items():
        if name.startswith("_"):
            continue
        if shape[-1] == "bf16":
            dbg_d[name] = dt("dbg_" + name, list(shape[:-1]), BF16, kind="ExternalOutput").ap()
        else:
            dbg_d[name] = dt("dbg_" + name, list(shape), F32, kind="ExternalOutput").ap()

    CW = 2056
    comb_g = dt("comb_g", [129 * P, CW], BF16).ap()
    wsc = {}
    for nm, shp in (("f1wi", [D, 2 * DFF]), ("f1wo", [DFF, D]), ("win", [D, INC]), ("wout", [D, D]),
                    ("f2wi", [D, 2 * DFF]), ("f2wo", [DFF, D])):
        wsc[nm] = dt("wsc_" + nm, shp, BF16).ap()

    es = ExitStack()
    with es:
        S = Sched(nc, es)

        used_names = {}

        def sbt(ctx, name, shape, dtype):
            k = used_names.get(name, 0)
            used_names[name] = k + 1
            return ctx.enter_context(nc.sbuf_tensor(name if k == 0 else f"{name}_v{k}", shape, dtype))

        cm = sbt(es, "cm", [P, 8, P], BF16)
        IDENT, TRI, D1, D4, NUI, NONES, D1A, D1B = (cm[:, i, :] for i in range(8))
        pv = sbt(es, "pv", [P, 16], F32)
        gB = sbt(es, "gB", [P, D], F32)
        lbB = sbt(es, "lbB", [P, 512], F32)
        omlB = sbt(es, "omlB", [P, 512], F32)
        ognB = sbt(es, "ognB", [P, 512], F32)
        qgB = sbt(es, "qgB", [P, 64], F32)
        kgB = sbt(es, "kgB", [P, 64], F32)
        ssqz = sbt(es, "ssqz", [P, 400], F32)
        rst = sbt(es, "rst", [P, 400], F32)
        ccs = sbt(es, "ccs", [P, 8], F32)
        pstk = ExitStack()
        PS = [pstk.enter_context(nc.psum_tensor(f"ps{i}", [P, 512], F32)) for i in range(7)]
        TPb = pstk.enter_context(nc.psum_tensor("tpb", [P, 8, P], BF16))
        ssq_next = [0]

        def new_col():
            c = ssq_next[0]
            ssq_next[0] += 1
            assert c < 400
            return c

        V, A, T, G, SP = nc.vector, nc.scalar, nc.tensor, nc.gpsimd, nc.sync

        S.dma("sp", cm[:], cmat_d.rearrange("p (k c) -> p k c", k=8), writes=["cm"], key="init")
        S.dma("sp", pv[:], pvec_d, writes=["pv"], key="init2")
        S.dma("sp", ognB[:], ogn_d.broadcast_to([P, 512]), writes=["ognB"], key="init2")
        S.dma("sp", qgB[:], qg_d.broadcast_to([P, 64]), writes=["qgB"], key="init2")
        S.dma("sp", kgB[:], kg_d.broadcast_to([P, 64]), writes=["kgB"], key="init2")
        S.op("dve", lambda: V.memset(ssqz[:], 0.0), writes=["ssqz"])
        with ExitStack() as c0x:
            lgt = sbt(c0x, "lgt", [P, 1024], F32)
            S.dma("sp", lgt[:], lbl_d.broadcast_to([P, 1024]), writes=["lgt"], key="init2")
            S.op("dve", lambda: V.tensor_tensor(out=lbB[:], in0=lgt[:, 512:1024], in1=lgt[:, 0:512], op=ALU.subtract),
                 reads=["lgt"], writes=["lbB"])
            S.op("act", lambda: A.activation(out=omlB[:], in_=lbB[:], func=AF.Exp), reads=["lbB"], writes=["omlB"])
            S.op("dve", lambda: V.tensor_scalar_add(out=lbB[:], in0=omlB[:], scalar1=1.0), reads=["omlB"], writes=["lbB"])
            S.op("dve", lambda: V.reciprocal(out=lbB[:], in_=lbB[:]), reads=["lbB"], writes=["lbB"])
            S.op("dve", lambda: V.tensor_scalar(out=omlB[:], in0=lbB[:], scalar1=-1.0, scalar2=1.0, op0=ALU.mult, op1=ALU.add),
                 reads=["lbB"], writes=["omlB"])
            S.barrier()
        S.op("dve", lambda: V.tensor_scalar_mul(out=qgB[:], in0=qgB[:], scalar1=0.125), reads=["qgB"], writes=["qgB"])

        for nm, src in (("f1wi", f1wi_d), ("f1wo", f1wo_d), ("win", win_d), ("wout", wout_d), ("f2wi", f2wi_d), ("f2wo", f2wo_d)):
            R, C = src.shape
            RT = 256
            for r0 in range(0, R, RT):
                rn = min(RT, R - r0)
                S.dma("pool", wsc[nm][r0:r0 + rn, :], src[r0:r0 + rn, :], writes=["wsc_" + nm], key="pc_" + nm)

        def rmsnorm_gen(src, srcn, xn, xnn, xnT_dst, tag, gt=None, gtn="gB"):
            gt = gB if gt is None else gt
            c = new_col()
            yield S.op("act", lambda: A.activation(out=xn[:], in_=src, func=AF.Square, accum_out=ssqz[:, c:c + 1]),
                       reads=list(srcn) + ["ssqz"], writes=[xnn, f"ssq{c}"])
            yield S.op("act", lambda: A.activation(out=rst[:, c:c + 1], in_=ssqz[:, c:c + 1], func=AF.Ln, scale=1.0 / D, bias=EPS),
                       reads=[f"ssq{c}"], writes=[f"rst{c}"])
            yield S.op("act", lambda: A.activation(out=rst[:, c:c + 1], in_=rst[:, c:c + 1], func=AF.Exp, scale=-0.5),
                       reads=[f"rst{c}"], writes=[f"rst{c}"])
            yield S.op("dve", lambda: V.scalar_tensor_tensor(out=xn[:], in0=src, scalar=rst[:, c:c + 1], in1=gt[:],
                                                             op0=ALU.mult, op1=ALU.mult),
                       reads=list(srcn) + [f"rst{c}", gtn], writes=[xnn])
            for kc in range(8):
                S.op("pe", lambda kc=kc: T.transpose(out=TPb[:, kc, :], in_=xn[:, kc * P:(kc + 1) * P], identity=IDENT),
                     reads=[xnn, "cm"], writes=["TPb"], sig=(kc == 7))
            yield S.op("act", lambda: A.copy(out=xnT_dst, in_=TPb[:]), reads=["TPb"], writes=[tag])

        def lockstep(gens):
            live = list(gens)
            while live:
                for g in list(live):
                    try:
                        next(g)
                    except StopIteration:
                        live.remove(g)

        def rmsnorm_T(src, srcn, xn, xnn, xnT_dst, tag, gt=None, gtn="gB"):
            lockstep([rmsnorm_gen(src, srcn, xn, xnn, xnT_dst, tag, gt, gtn)])

        def dbg_dump(name, ap_sb, reads):
            if name in dbg_d:
                S.dma("sp", dbg_d[name], ap_sb, reads=reads, key="dbg")

        ffc = {"fc": 0, "gc": 0, "yc": 0}

        def ffn_alloc(cx, width):
            t = {}
            t["W2"] = sbt(cx, "W2", [P, NF, D], BF16)
            t["hid"] = sbt(cx, "hid", [P, NF, width], BF16)
            t["xnT"] = sbt(cx, "xnT", [P, 8, width], BF16)
            t["W1"] = [sbt(cx, f"W1_{i}", [P, 8, 256], BF16) for i in range(3)]
            t["xn"] = [sbt(cx, f"xn{i}", [P, D], BF16) for i in range(2)]
            t["sg"] = [sbt(cx, f"sg{i}", [P, 512], F32) for i in range(2)]
            return t

        def ffn_load_w2(t, wo_name):
            wo_v = wsc[wo_name].rearrange("(f p) n -> p f n", p=P)
            for i in range(0, NF, 6):
                j = min(NF, i + 6)
                S.dma("sp", t["W2"][:, i:j, :], wo_v[:, i:j, :], reads=["wsc_" + wo_name], writes=["W2"], key="w2")

        def ffn_pass(t, pb, Hs, hpfx, wi_name, gt=None, gtn="gB", out_d=None):
            W2, hid, xnT, W1, xn, sg = t["W2"], t["hid"], t["xnT"], t["W1"], t["xn"], t["sg"]
            wi_v = wsc[wi_name].rearrange("(kc p) n -> p kc n", p=P)
            Tn = len(pb) * P
            for i0 in range(0, len(pb), 2):
                lockstep([rmsnorm_gen(Hs[:, pb[i], :], [f"{hpfx}{pb[i]}", hpfx], xn[i % 2], f"xn{i % 2}", xnT[:, :, i * P:(i + 1) * P],
                                      "xnT", gt, gtn) for i in range(i0, min(i0 + 2, len(pb)))])
            groups = [(t0, min(512, Tn - t0)) for t0 in range(0, Tn, 512)]
            for f in range(NF):
                sl = ffc["fc"] % 3
                ffc["fc"] += 1
                S.dma("sp", W1[sl][:, :, 0:P], wi_v[:, :, f * P:(f + 1) * P], reads=["wsc_" + wi_name], writes=[f"W1_{sl}"], key=f"w1_{sl}")
                S.dma("sp", W1[sl][:, :, P:2 * P], wi_v[:, :, DFF + f * P:DFF + (f + 1) * P],
                      reads=["wsc_" + wi_name], writes=[f"W1_{sl}"], key=f"w1_{sl}")
                for (t0, tn) in groups:
                    gs = ffc["gc"] % 2
                    ffc["gc"] += 1
                    Gp, Up = PS[gs], PS[2 + gs]
                    for kc in range(8):
                        S.op("pe", lambda kc=kc: T.matmul(Gp[:, 0:tn], lhsT=W1[sl][:, kc, 0:P], rhs=xnT[:, kc, t0:t0 + tn],
                                                          start=(kc == 0), stop=(kc == 7)),
                             reads=[f"W1_{sl}", "xnT"], writes=[f"ps{gs}"], sig=False)
                    for kc in range(8):
                        S.op("pe", lambda kc=kc: T.matmul(Up[:, 0:tn], lhsT=W1[sl][:, kc, P:2 * P], rhs=xnT[:, kc, t0:t0 + tn],
                                                          start=(kc == 0), stop=(kc == 7)),
                             reads=[f"W1_{sl}", "xnT"], writes=[f"ps{2 + gs}"], sig=(kc == 7))
                    S.op("act", lambda: A.activation(out=sg[gs][:, 0:tn], in_=Gp[:, 0:tn], func=AF.Silu),
                         reads=[f"ps{gs}"], writes=[f"sg{gs}"])
                    S.op("dve", lambda: V.tensor_tensor(out=hid[:, f, t0:t0 + tn], in0=sg[gs][:, 0:tn], in1=Up[:, 0:tn],
                                                        op=ALU.mult),
                         reads=[f"sg{gs}", f"ps{2 + gs}"], writes=["hid"])
            for i, b in enumerate(pb):
                for half in range(2):
                    ys = 4 + (ffc["yc"] % 2)
                    ffc["yc"] += 1
                    for f in range(NF):
                        S.op("pe", lambda f=f: T.matmul(PS[ys][:], lhsT=hid[:, f, i * P:(i + 1) * P],
                                                        rhs=W2[:, f, half * 512:(half + 1) * 512],
                                                        start=(f == 0), stop=(f == NF - 1)),
                             reads=["hid", "W2"], writes=[f"ps{ys}"], sig=(f == NF - 1))
                    hs = Hs[:, b, half * 512:(half + 1) * 512]
                    S.op("dve", lambda: V.scalar_tensor_tensor(out=hs, in0=PS[ys][:], scalar=0.5, in1=hs,
                                                               op0=ALU.mult, op1=ALU.add),
                         reads=[f"ps{ys}", f"{hpfx}{b}", hpfx], writes=[f"{hpfx}{b}"])
                if out_d is not None:
                    S.dma("sp", out_d[b - 1], Hs[:, b, :], reads=[f"{hpfx}{b}"], key="out")

        def ffn(passes, gain_d, wi_name, wo_name, out_d=None):
            with ExitStack() as cx:
                t = ffn_alloc(cx, 1024)
                S.dma("sp", gB[:], gain_d.broadcast_to([P, D]), writes=["gB"], key="gain")
                ffn_load_w2(t, wo_name)
                for pb in passes:
                    ffn_pass(t, pb, H, "H", wi_name, out_d=out_d)
                S.barrier()

        comb_v = comb_g.rearrange("(g p) c -> g p c", p=P)
        with ExitStack() as cx:
            t = ffn_alloc(cx, 512)
            Ht = sbt(cx, "Ht", [P, 4, D], F32)
            gBm = sbt(cx, "gBm", [P, D], F32)
            Wf = sbt(cx, "Wf", [P, 8, 2048], BF16)
            win_v0 = wsc["win"].rearrange("(kc p) n -> p kc n", p=P)
            S.dma("sp", gB[:], f1n_d.broadcast_to([P, D]), writes=["gB"], key="gain")
            S.dma("sp", gBm[:], mixn_d.broadcast_to([P, D]), writes=["gBm"], key="gain")
            for i, c0 in enumerate((512, 1024, 2560, 3072)):
                S.dma("sp", Wf[:, :, i * 512:(i + 1) * 512], win_v0[:, :, c0:c0 + 512], reads=["wsc_win"], writes=["Wf"], key="win")
            ffn_load_w2(t, "f1wo")
            xnm = t["xn"]
            TS = []
            for k in range(2):
                d = {}
                d["xnT1"] = sbt(cx, f"xnT1f{k}", [P, 8, P], BF16)
                for nm in ("sgm", "ff", "kf", "et", "t32"):
                    d[nm] = sbt(cx, f"{nm}F{k}", [P, 512], F32)
                for nm in ("lfh", "lfl", "kpp", "vb", "nrm"):
                    d[nm] = sbt(cx, f"{nm}F{k}", [P, 512], BF16)
                d["bst"] = sbt(cx, f"bstF{k}", [P, 516], F32)
                d["kvst"] = sbt(cx, f"kvstF{k}", [P, 8, P], BF16)
                d["s8"] = sbt(cx, f"s8F{k}", [P, 8], F32)
                TS.append(d)
            onec = sbt(cx, "onecF", [P, 1], BF16)
            S.op("dve", lambda: V.memset(onec[:], 1.0), writes=["onecF"])

            def p2_pre(i, gb, k):
                d = TS[k]
                xnT1, sgm, ff, kf, et, t32 = d["xnT1"], d["sgm"], d["ff"], d["kf"], d["et"], d["t32"]
                lfh, lfl, kpp, vb, nrm, bs, kv, s8 = d["lfh"], d["lfl"], d["kpp"], d["vb"], d["nrm"], d["bst"], d["kvst"], d["s8"]
                N = lambda nm: f"{nm}F{k}"
                QA, QB, QC = PS[3 * k], PS[3 * k + 1], PS[3 * k + 2]
                qa, qb, qc = f"ps{3 * k}", f"ps{3 * k + 1}", f"ps{3 * k + 2}"

                def projF(dst, dname, c0):
                    for kc in range(8):
                        yield S.op("pe", lambda kc=kc: T.matmul(dst[:], lhsT=xnT1[:, kc, :], rhs=Wf[:, kc, c0:c0 + 512],
                                                                start=(kc == 0), stop=(kc == 7)),
                                   reads=[N("xnT1"), "Wf"], writes=[dname], sig=(kc == 7))

                c = new_col()
                src = Ht[:, i, :]
                yield S.op("act", lambda: A.activation(out=xnm[k][:], in_=src, func=AF.Square, accum_out=ssqz[:, c:c + 1]),
                           reads=[f"Ht{i}", "Ht", "ssqz"], writes=[f"xn{k}", f"ssq{c}"])
                yield S.op("act", lambda: A.activation(out=rst[:, c:c + 1], in_=ssqz[:, c:c + 1], func=AF.Ln, scale=1.0 / D, bias=EPS),
                           reads=[f"ssq{c}"], writes=[f"rst{c}"])
                yield S.op("act", lambda: A.activation(out=rst[:, c:c + 1], in_=rst[:, c:c + 1], func=AF.Exp, scale=-0.5),
                           reads=[f"rst{c}"], writes=[f"rst{c}"])
                yield S.op("dve", lambda: V.scalar_tensor_tensor(out=xnm[k][:], in0=src, scalar=rst[:, c:c + 1], in1=gBm[:],
                                                                 op0=ALU.mult, op1=ALU.mult),
                           reads=[f"Ht{i}", "Ht", f"rst{c}", "gBm"], writes=[f"xn{k}"])
                for kc in range(8):
                    S.op("pe", lambda kc=kc: T.transpose(out=TPb[:, kc, :], in_=xnm[k][:, kc * P:(kc + 1) * P], identity=IDENT),
                         reads=[f"xn{k}", "cm"], writes=["TPb"], sig=(kc == 7))
                yield S.op("act", lambda: A.copy(out=xnT1[:], in_=TPb[:]), reads=["TPb"], writes=[N("xnT1")])

            def p2_hg(i, gb, k):
                d = TS[k]
                xnT1, sgm, ff, kf, et, t32 = d["xnT1"], d["sgm"], d["ff"], d["kf"], d["et"], d["t32"]
                lfh, lfl, kpp, vb, nrm, bs, kv, s8 = d["lfh"], d["lfl"], d["kpp"], d["vb"], d["nrm"], d["bst"], d["kvst"], d["s8"]
                N = lambda nm: f"{nm}F{k}"
                QA, QB, QC = PS[3 * k], PS[3 * k + 1], PS[3 * k + 2]
                qa, qb, qc = f"ps{3 * k}", f"ps{3 * k + 1}", f"ps{3 * k + 2}"

                def projF(dst, dname, c0):
                    for kc in range(8):
                        yield S.op("pe", lambda kc=kc: T.matmul(dst[:], lhsT=xnT1[:, kc, :], rhs=Wf[:, kc, c0:c0 + 512],
                                                                start=(kc == 0), stop=(kc == 7)),
                                   reads=[N("xnT1"), "Wf"], writes=[dname], sig=(kc == 7))

                yield from projF(QA, qa, 0)
                yield from projF(QB, qb, 512)
                yield S.op("act", lambda: A.activation(out=sgm[:], in_=QA[:], func=AF.Sigmoid), reads=[qa], writes=[N("sgm")])
                yield S.op("dve", lambda: V.tensor_tensor(out=ff[:], in0=sgm[:], in1=omlB[:], op=ALU.mult), reads=[N("sgm"), "omlB"], writes=[N("ff")])
                yield S.op("dve", lambda: V.tensor_tensor(out=ff[:], in0=ff[:], in1=lbB[:], op=ALU.add), reads=[N("ff"), "lbB"], writes=[N("ff")])
                yield S.op("act", lambda: A.activation(out=ff[:], in_=ff[:], func=AF.Ln), reads=[N("ff")], writes=[N("ff")])
                yield S.op("dve", lambda: V.tensor_scalar(out=kf[:], in0=sgm[:], scalar1=-1.0, scalar2=1.0, op0=ALU.mult, op1=ALU.add),
                           reads=[N("sgm")], writes=[N("kf")])
                yield S.op("dve", lambda: V.tensor_tensor(out=kf[:], in0=kf[:], in1=omlB[:], op=ALU.mult), reads=[N("kf"), "omlB"], writes=[N("kf")])
                if gb == 0:
                    yield S.op("dve", lambda: V.tensor_scalar_mul(out=ff[:], in0=ff[:], scalar1=pv[:, 0:1]), reads=[N("ff"), "pv"], writes=[N("ff")])
                    yield S.op("dve", lambda: V.tensor_scalar_mul(out=kf[:], in0=kf[:], scalar1=pv[:, 0:1]), reads=[N("kf"), "pv"], writes=[N("kf")])
                yield S.op("dve", lambda: V.tensor_copy(out=lfh[:], in_=ff[:]), reads=[N("ff")], writes=[N("lfh")])
                yield S.op("dve", lambda: V.tensor_tensor(out=lfl[:], in0=ff[:], in1=lfh[:], op=ALU.subtract), reads=[N("ff"), N("lfh")], writes=[N("lfl")])
                S.op("pe", lambda: T.matmul(QC[:], lhsT=D4, rhs=lfh[:], start=True, stop=False), reads=["cm", N("lfh")], writes=[qc], sig=False)
                yield S.op("pe", lambda: T.matmul(QC[:], lhsT=D4, rhs=lfl[:], start=False, stop=True), reads=["cm", N("lfl")], writes=[qc], sig=True)
                for h in range(4):
                    S.op("pe", lambda h=h: T.matmul(QA[:, h:h + 1], lhsT=lfh[:, h * P:(h + 1) * P], rhs=onec[:], start=True, stop=False),
                         reads=[N("lfh"), "onecF"], writes=[qa], sig=False)
                    S.op("pe", lambda h=h: T.matmul(QA[:, h:h + 1], lhsT=lfl[:, h * P:(h + 1) * P], rhs=onec[:], start=False, stop=True),
                         reads=[N("lfl"), "onecF"], writes=[qa], sig=(h == 3))
                yield
                yield S.op("act", lambda: A.activation(out=bs[:, 512:516], in_=QA[:, 0:4], func=AF.Exp), reads=[qa], writes=[N("bst")])
                yield S.op("act", lambda: A.activation(out=et[:], in_=QC[:], func=AF.Exp), reads=[qc], writes=[N("et")])
                yield S.op("dve", lambda: V.tensor_tensor(out=kpp[:], in0=kf[:], in1=et[:], op=ALU.mult), reads=[N("kf"), N("et")], writes=[N("kpp")])
                yield S.op("act", lambda: A.copy(out=vb[:], in_=QB[:]), reads=[qb], writes=[N("vb")])
                for h in range(4):
                    S.op("pe", lambda h=h: T.matmul(QC[:, h * P:(h + 1) * P], lhsT=kpp[:, h * P:(h + 1) * P], rhs=vb[:, h * P:(h + 1) * P],
                                                    start=True, stop=True),
                         reads=[N("kpp"), N("vb")], writes=[qc], sig=(h == 3))
                yield
                yield S.op("dve", lambda: V.tensor_copy(out=bs[:, 0:512], in_=QC[:]), reads=[qc], writes=[N("bst")])
                yield S.dma("sp", comb_v[gb][:, 0:1032], bs[:].bitcast(BF16), reads=[N("bst")], writes=["comb_g"], key=f"bastF{k}")

            def p2_sb(i, gb, k):
                d = TS[k]
                xnT1, sgm, ff, kf, et, t32 = d["xnT1"], d["sgm"], d["ff"], d["kf"], d["et"], d["t32"]
                lfh, lfl, kpp, vb, nrm, bs, kv, s8 = d["lfh"], d["lfl"], d["kpp"], d["vb"], d["nrm"], d["bst"], d["kvst"], d["s8"]
                N = lambda nm: f"{nm}F{k}"
                QA, QB, QC = PS[3 * k], PS[3 * k + 1], PS[3 * k + 2]
                qa, qb, qc = f"ps{3 * k}", f"ps{3 * k + 1}", f"ps{3 * k + 2}"

                def projF(dst, dname, c0):
                    for kc in range(8):
                        yield S.op("pe", lambda kc=kc: T.matmul(dst[:], lhsT=xnT1[:, kc, :], rhs=Wf[:, kc, c0:c0 + 512],
                                                                start=(kc == 0), stop=(kc == 7)),
                                   reads=[N("xnT1"), "Wf"], writes=[dname], sig=(kc == 7))

                QS, qs = PS[6], "ps6"
                yield from projF(QS, qs, 1024)
                yield S.op("act", lambda: A.activation(out=t32[:], in_=QS[:], func=AF.Square), reads=[qs], writes=[N("t32")])
                yield S.op("dve", lambda: V.tensor_reduce(out=s8[:], in_=t32[:].rearrange("p (h d) -> p h d", h=8), axis=AX.X, op=ALU.add),
                           reads=[N("t32")], writes=[N("s8")])
                yield S.op("act", lambda: A.activation(out=s8[:], in_=s8[:], func=AF.Ln, scale=1.0 / 64, bias=EPS), reads=[N("s8")], writes=[N("s8")])
                yield S.op("act", lambda: A.activation(out=s8[:], in_=s8[:], func=AF.Exp, scale=-0.5), reads=[N("s8")], writes=[N("s8")])
                yield S.op("dve", lambda: V.tensor_tensor(out=t32[:].rearrange("p (h d) -> p h d", h=8),
                                                          in0=QS[:].rearrange("p (h d) -> p h d", h=8),
                                                          in1=s8[:].unsqueeze(2).to_broadcast([P, 8, 64]), op=ALU.mult),
                           reads=[qs, N("s8")], writes=[N("t32")])
                yield S.op("dve", lambda: V.tensor_tensor(out=nrm[:].rearrange("p (h d) -> p h d", h=8),
                                                          in0=t32[:].rearrange("p (h d) -> p h d", h=8),
                                                          in1=kgB[:].unsqueeze(1).to_broadcast([P, 8, 64]), op=ALU.mult),
                           reads=[N("t32"), "kgB"], writes=[N("nrm")])
                yield from projF(QS, qs, 1536)
                yield S.op("act", lambda: A.copy(out=kv[:, 4:8, :], in_=QS[:].rearrange("p (a c) -> p a c", a=4)), reads=[qs], writes=[N("kvst")])
                for hp in range(4):
                    S.op("pe", lambda hp=hp: T.transpose(out=TPb[:, hp, :], in_=nrm[:, hp * P:(hp + 1) * P], identity=IDENT),
                         reads=[N("nrm"), "cm"], writes=["TPb"], sig=(hp == 3))
                yield S.op("dve", lambda: V.tensor_copy(out=kv[:, 0:4, :], in_=TPb[:, 0:4, :]), reads=["TPb"], writes=[N("kvst")])
                yield S.dma("sp", comb_v[gb][:, 1032:2056].rearrange("p (a c) -> p a c", a=8), kv[:], reads=[N("kvst")], writes=["comb_g"],
                            key=f"kvstF{k}")


            nsp = (129 + 3) // 4

            def load_ht(sp, i):
                gb = 4 * sp + i
                if sp < nsp and gb < 129:
                    S.dma("sp", Ht[:, i, :], xall_d[gb], writes=[f"Ht{i}"], key=f"xt{i}")

            for i in range(4):
                load_ht(0, i)
            for sp in range(nsp):
                gbs = list(range(4 * sp, min(4 * sp + 4, 129)))
                n = len(gbs)
                ffn_pass(t, list(range(n)), Ht, "Ht", "f1wi")
                for i0 in range(0, 4, 2):
                    if i0 < n:
                        rng = range(i0, min(i0 + 2, n))
                        lockstep([p2_pre(i, gbs[i], i - i0) for i in rng])
                        def sb_seq(rng=rng, i0=i0):
                            for i in rng:
                                yield from p2_sb(i, gbs[i], i - i0)
                        lockstep([p2_hg(i, gbs[i], i - i0) for i in rng] + [sb_seq()])
                    load_ht(sp + 1, i0)
                    load_ht(sp + 1, i0 + 1)
            S.barrier()

        H = sbt(es, "H", [P, NBL, D], F32)
        S.dma("sp", H[:], x_d.rearrange("b p d -> p b d"), writes=["H"], key="x")

        skip = dbg.get("_skip", ())
        if "ffn1" not in skip:
            ffn([list(range(1, 9)), list(range(9, 17))], f1n_d, "f1wi", "f1wo")
        if "h1" in dbg_d and dbg.get("_stop") == 1:
            dbg_dump("h1", H[:], ["H"] + [f"H{b}" for b in range(NBL)])

        if dbg.get("_stop") == 1:
            S.barrier()
            S.final_wait(["dbg"])
            return nc

        with ExitStack() as c1:
            QT = sbt(c1, "QT", [P, 4, NQ * P], BF16)
            cA = ExitStack()
            qppT = sbt(cA, "qppT", [P, 4, NQ * P], BF16)
            oin = sbt(cA, "oin", [P, NQ, 512], BF16)
            S.dma("sp", gB[:], mixn_d.broadcast_to([P, D]), writes=["gB"], key="gain")
            win_v = wsc["win"].rearrange("(kc p) n -> p kc n", p=P)

            with ExitStack() as c2:
                Wh = sbt(c2, "Wh", [P, 8, 1536], BF16)
                for i in range(3):
                    S.dma("sp", Wh[:, :, i * 512:(i + 1) * 512], win_v[:, :, i * 512:(i + 1) * 512], reads=["wsc_win"], writes=["Wh"], key="win")
                xn = sbt(c2, "xn_a", [P, D], BF16)
                xnT1 = sbt(c2, "xnT1", [P, 8, P], BF16)
                qf = sbt(c2, "qf", [P, 512], F32)
                sgm = sbt(c2, "sgm", [P, 512], F32)
                ff = sbt(c2, "ff", [P, 512], F32)
                kf = sbt(c2, "kf", [P, 512], F32)
                et = [sbt(c2, f"et{i}", [P, 512], F32) for i in range(2)]
                lfh = sbt(c2, "lfh", [P, 512], BF16)
                lfl = sbt(c2, "lfl", [P, 512], BF16)
                qp = sbt(c2, "qp", [P, 512], BF16)
                kp = sbt(c2, "kp", [P, 512], BF16)
                qpp = sbt(c2, "qpp", [P, 512], BF16)
                kpp = sbt(c2, "kpp", [P, 512], BF16)
                vb = sbt(c2, "vb", [P, 512], BF16)
                qkT3 = [sbt(c2, f"qkT{i}", [P, 8, P], BF16) for i in range(3)]
                qpv = [sbt(c2, f"qpv{i}", [P, 512], BF16) for i in range(3)]
                kpv = [sbt(c2, f"kpv{i}", [P, 512], BF16) for i in range(3)]
                ATm = sbt(c2, "ATm", [P, 4, P], BF16)
                bst = [sbt(c2, f"bst{i}", [P, 516], F32) for i in range(2)]
                onec = sbt(c2, "onec", [P, 1], BF16)
                S.op("dve", lambda: V.memset(onec[:], 1.0), writes=["onec"])
                UA, UB, C1, C3, C4, PA = PS[0], PS[1], PS[2], PS[3], PS[4], PS[5]

                def proj(dst, dname, c0):
                    for kc in range(8):
                        S.op("pe", lambda kc=kc: T.matmul(dst[:], lhsT=xnT1[:, kc, :], rhs=Wh[:, kc, c0:c0 + 512],
                                                          start=(kc == 0), stop=(kc == 7)),
                             reads=["xnT1", "Wh"], writes=[dname], sig=(kc == 7))

                for b in range(1, NBL if "p2a" not in skip else 0):
                    rmsnorm_T(H[:, b, :], [f"H{b}", "H"], xn, "xn_a", xnT1[:], "xnT1")
                    proj(UA, "ps0", 0)
                    proj(UB, "ps1", 512)
                    S.op("act", lambda: A.activation(out=qf[:], in_=UA[:], func=AF.Silu), reads=["ps0"], writes=["qf"])
                    S.op("act", lambda: A.activation(out=sgm[:], in_=UB[:], func=AF.Sigmoid), reads=["ps1"], writes=["sgm"])
                    proj(UA, "ps0", 1024)
                    S.op("dve", lambda: V.tensor_tensor(out=ff[:], in0=sgm[:], in1=omlB[:], op=ALU.mult), reads=["sgm", "omlB"], writes=["ff"])
                    S.op("dve", lambda: V.tensor_tensor(out=ff[:], in0=ff[:], in1=lbB[:], op=ALU.add), reads=["ff", "lbB"], writes=["ff"])
                    S.op("act", lambda: A.activation(out=ff[:], in_=ff[:], func=AF.Ln), reads=["ff"], writes=["ff"])
                    S.op("dve", lambda: V.tensor_scalar(out=kf[:], in0=sgm[:], scalar1=-1.0, scalar2=1.0, op0=ALU.mult, op1=ALU.add),
                         reads=["sgm"], writes=["kf"])
                    S.op("dve", lambda: V.tensor_tensor(out=kf[:], in0=kf[:], in1=omlB[:], op=ALU.mult), reads=["kf", "omlB"], writes=["kf"])
                    if b == 0:
                        S.op("dve", lambda: V.tensor_scalar_mul(out=ff[:], in0=ff[:], scalar1=pv[:, 0:1]), reads=["ff", "pv"], writes=["ff"])
                        S.op("dve", lambda: V.tensor_scalar_mul(out=kf[:], in0=kf[:], scalar1=pv[:, 0:1]), reads=["kf", "pv"], writes=["kf"])
                    S.op("dve", lambda: V.tensor_copy(out=lfh[:], in_=ff[:]), reads=["ff"], writes=["lfh"])
                    S.op("dve", lambda: V.tensor_tensor(out=lfl[:], in0=ff[:], in1=lfh[:], op=ALU.subtract), reads=["ff", "lfh"], writes=["lfl"])
                    for (dst, dn, Mx) in ((C1, "ps2", D1), (PS[5], "ps5", D1A), (PS[6], "ps6", D1B), (C3, "ps3", TRI)):
                        S.op("pe", lambda dst=dst, Mx=Mx: T.matmul(dst[:], lhsT=Mx, rhs=lfh[:], start=True, stop=False),
                             reads=["cm", "lfh"], writes=[dn], sig=False)
                        S.op("pe", lambda dst=dst, Mx=Mx: T.matmul(dst[:], lhsT=Mx, rhs=lfl[:], start=False, stop=True),
                             reads=["cm", "lfl"], writes=[dn], sig=True)
                    mA, mB = pv[:, 11:12], pv[:, 12:13]
                    for vi, (src, sn, qbias, kbias) in enumerate(((PS[5], "ps5", mA, mA), (PS[6], "ps6", mB, mB), (C1, "ps2", mB, mA))):
                        S.op("act", lambda src=src, qbias=qbias: A.activation(out=et[0][:], in_=src[:], func=AF.Exp, bias=qbias),
                             reads=[sn, "pv"], writes=["et0"])
                        S.op("dve", lambda vi=vi: V.tensor_tensor(out=qpv[vi][:], in0=qf[:], in1=et[0][:], op=ALU.mult),
                             reads=["qf", "et0"], writes=[f"qpv{vi}"])
                        S.op("act", lambda src=src, kbias=kbias: A.activation(out=et[1][:], in_=src[:], func=AF.Exp, scale=-1.0, bias=kbias),
                             reads=[sn, "pv"], writes=["et1"])
                        S.op("dve", lambda vi=vi: V.tensor_tensor(out=kpv[vi][:], in0=kf[:], in1=et[1][:], op=ALU.mult),
                             reads=["kf", "et1"], writes=[f"kpv{vi}"])
                    S.op("act", lambda: A.activation(out=et[0][:], in_=C3[:], func=AF.Exp), reads=["ps3"], writes=["et0"])
                    S.op("dve", lambda: V.tensor_tensor(out=qpp[:], in0=qf[:], in1=et[0][:], op=ALU.mult), reads=["qf", "et0"], writes=["qpp"])
                    S.op("act", lambda: A.copy(out=vb[:], in_=UA[:]), reads=["ps0"], writes=["vb"])
                    if b >= 1:
                        for vi in range(3):
                            for h in range(4):
                                S.op("pe", lambda h=h, vi=vi: T.transpose(out=TPb[:, h, :], in_=qpv[vi][:, h * P:(h + 1) * P], identity=IDENT),
                                     reads=[f"qpv{vi}", "cm"], writes=["TPb"], sig=False)
                            for h in range(4):
                                S.op("pe", lambda h=h, vi=vi: T.transpose(out=TPb[:, 4 + h, :], in_=kpv[vi][:, h * P:(h + 1) * P], identity=IDENT),
                                     reads=[f"kpv{vi}", "cm"], writes=["TPb"], sig=(h == 3))
                            S.op("act", lambda vi=vi: A.copy(out=qkT3[vi][:], in_=TPb[:]), reads=["TPb"], writes=[f"qkT{vi}"])
                        AT3 = C1[:].rearrange("p (h t) -> p h t", h=4)
                        for h in range(4):
                            for vi in range(3):
                                S.op("pe", lambda h=h, vi=vi: T.matmul(C1[:, h * P:(h + 1) * P], lhsT=qkT3[vi][:, 4 + h, :], rhs=qkT3[vi][:, h, :],
                                                                       start=(vi == 0), stop=(vi == 2)),
                                     reads=[f"qkT{vi}"], writes=["ps2"], sig=(h == 3 and vi == 2))
                        S.op("dve", lambda: V.tensor_tensor(out=ATm[:], in0=AT3, in1=TRI.unsqueeze(1).to_broadcast([P, 4, P]), op=ALU.mult),
                             reads=["ps2", "cm"], writes=["ATm"])
                        for h in range(4):
                            S.op("pe", lambda h=h: T.matmul(C3[:, h * P:(h + 1) * P], lhsT=ATm[:, h, :], rhs=vb[:, h * P:(h + 1) * P], start=True, stop=True),
                                 reads=["ATm", "vb"], writes=["ps3"], sig=(h == 3))
                        S.op("act", lambda: A.copy(out=oin[:, b - 1, :], in_=C3[:]), reads=["ps3"], writes=["oin"])
                        for h in range(4):
                            S.op("pe", lambda h=h: T.transpose(out=TPb[:, h, :], in_=qpp[:, h * P:(h + 1) * P], identity=IDENT),
                                 reads=["qpp", "cm"], writes=["TPb"], sig=(h == 3))
                        S.op("act", lambda: A.copy(out=qppT[:, :, (b - 1) * P:b * P], in_=TPb[:, 0:4, :]), reads=["TPb"], writes=["qppT"])
                S.barrier()
            if dbg.get("_stop") == 21:
                dbg_dump("oin", oin[:], ["oin"])
                dbg_dump("qppT", qppT[:], ["qppT"])
                S.barrier()
                S.final_wait(["dbg"])
                return nc

            with ExitStack() as c2:
                Wq = sbt(c2, "Wq", [P, 8, 512], BF16)
                S.dma("sp", Wq[:], win_v[:, :, 2048:2560], reads=["wsc_win"], writes=["Wq"], key="win")
                QS = []
                for k in range(2):
                    QS.append(dict(xn=sbt(c2, f"xn_b{k}", [P, D], BF16), xnT1=sbt(c2, f"xnT1b{k}", [P, 8, P], BF16),
                                   t32=sbt(c2, f"t32b{k}", [P, 512], F32), nrm=sbt(c2, f"nrmb{k}", [P, 512], BF16),
                                   s8=sbt(c2, f"s8b{k}", [P, 8], F32)))

                def p2b(b, k):
                    d = QS[k]
                    xn, xnT1, t32, nrm, s8 = d["xn"], d["xnT1"], d["t32"], d["nrm"], d["s8"]
                    UQ, uq = PS[k], f"ps{k}"
                    yield from rmsnorm_gen(H[:, b, :], [f"H{b}", "H"], xn, f"xn_b{k}", xnT1[:], f"xnT1b{k}")
                    for kc in range(8):
                        S.op("pe", lambda kc=kc: T.matmul(UQ[:], lhsT=xnT1[:, kc, :], rhs=Wq[:, kc, :], start=(kc == 0), stop=(kc == 7)),
                             reads=[f"xnT1b{k}", "Wq"], writes=[uq], sig=(kc == 7))
                    yield
                    yield S.op("act", lambda: A.activation(out=t32[:], in_=UQ[:], func=AF.Square), reads=[uq], writes=[f"t32b{k}"])
                    yield S.op("dve", lambda: V.tensor_reduce(out=s8[:], in_=t32[:].rearrange("p (h d) -> p h d", h=8), axis=AX.X, op=ALU.add),
                               reads=[f"t32b{k}"], writes=[f"s8b{k}"])
                    yield S.op("act", lambda: A.activation(out=s8[:], in_=s8[:], func=AF.Ln, scale=1.0 / 64, bias=EPS), reads=[f"s8b{k}"], writes=[f"s8b{k}"])
                    yield S.op("act", lambda: A.activation(out=s8[:], in_=s8[:], func=AF.Exp, scale=-0.5), reads=[f"s8b{k}"], writes=[f"s8b{k}"])
                    yield S.op("dve", lambda: V.tensor_tensor(out=t32[:].rearrange("p (h d) -> p h d", h=8),
                                                              in0=UQ[:].rearrange("p (h d) -> p h d", h=8),
                                                              in1=s8[:].unsqueeze(2).to_broadcast([P, 8, 64]), op=ALU.mult),
                               reads=[uq, f"s8b{k}"], writes=[f"t32b{k}"])
                    yield S.op("dve", lambda: V.tensor_tensor(out=nrm[:].rearrange("p (h d) -> p h d", h=8),
                                                              in0=t32[:].rearrange("p (h d) -> p h d", h=8),
                                                              in1=qgB[:].unsqueeze(1).to_broadcast([P, 8, 64]), op=ALU.mult),
                               reads=[f"t32b{k}", "qgB"], writes=[f"nrmb{k}"])
                    for hp in range(4):
                        S.op("pe", lambda hp=hp: T.transpose(out=TPb[:, hp, :], in_=nrm[:, hp * P:(hp + 1) * P], identity=IDENT),
                             reads=[f"nrmb{k}", "cm"], writes=["TPb"], sig=(hp == 3))
                    yield S.op("act", lambda: A.copy(out=QT[:, :, (b - 1) * P:b * P], in_=TPb[:, 0:4, :]), reads=["TPb"], writes=["QT"])

                if "p2b" not in skip:
                    for b0 in range(1, NBL, 2):
                        lockstep([p2b(b, b - b0) for b in range(b0, min(b0 + 2, NBL))])
                S.barrier()
            if dbg.get("_stop") == 22:
                dbg_dump("QT", QT[:], ["QT"])
                S.barrier()
                S.final_wait(["dbg"])
                return nc
            S.barrier()
            with ExitStack() as c2:
                Wg = sbt(c2, "Wg", [P, 8, 512], BF16)
                Wo1 = sbt(c2, "Wo1", [P, 4, D], BF16)
                S.dma("sp", Wg[:], win_v[:, :, 1536:2048], reads=["wsc_win"], writes=["Wg"], key="win")
                S.dma("sp", Wo1[:], wsc["wout"][0:512, :].rearrange("(h p) n -> p h n", p=P), reads=["wsc_wout"], writes=["Wo1"], key="wo")
                Sst = sbt(c2, "Sst", [P, 512], F32)
                Ssel = [sbt(c2, f"Ssel{i}", [P, 512], F32) for i in range(2)]
                Sb = sbt(c2, "Sb", [P, 512], BF16)
                Bl = [sbt(c2, f"Bl{i}", [P, 516], F32) for i in range(3)]
                xn = sbt(c2, "xn_c", [P, D], BF16)
                xnT1 = sbt(c2, "xnT1c", [P, 8, P], BF16)
                sgl = sbt(c2, "sgl", [P, 512], F32)
                ot = sbt(c2, "ot", [P, 512], F32)
                t32 = sbt(c2, "t32c", [P, 512], F32)
                ob = sbt(c2, "ob", [P, 512], BF16)
                obT = sbt(c2, "obT", [P, 4, P], BF16)
                s4 = sbt(c2, "s4", [P, 4], F32)
                S.op("dve", lambda: V.memset(Sst[:], 0.0), writes=["Sst"])
                UA, OI = PS[0], PS[1]

                def block_out(l, sel, seln):
                    S.op("dve", lambda: V.tensor_copy(out=Sb[:], in_=sel[:]), reads=[seln], writes=["Sb"])
                    rmsnorm_T(H[:, l, :], [f"H{l}", "H"], xn, "xn_c", xnT1[:], "xnT1c")
                    for kc in range(8):
                        S.op("pe", lambda kc=kc: T.matmul(UA[:], lhsT=xnT1[:, kc, :], rhs=Wg[:, kc, :], start=(kc == 0), stop=(kc == 7)),
                             reads=["xnT1c", "Wg"], writes=["ps0"], sig=(kc == 7))
                    S.op("act", lambda: A.activation(out=sgl[:], in_=UA[:], func=AF.Silu), reads=["ps0"], writes=["sgl"])
                    for h in range(4):
                        S.op("pe", lambda h=h: T.matmul(OI[:, h * P:(h + 1) * P], lhsT=qppT[:, h, (l - 1) * P:l * P], rhs=Sb[:, h * P:(h + 1) * P],
                                                        start=True, stop=True),
                             reads=["qppT", "Sb"], writes=["ps1"], sig=(h == 3))
                    S.op("dve", lambda: V.tensor_tensor(out=ot[:], in0=OI[:], in1=oin[:, l - 1, :], op=ALU.add), reads=["ps1", "oin"], writes=["ot"])
                    S.op("act", lambda: A.activation(out=t32[:], in_=ot[:], func=AF.Square), reads=["ot"], writes=["t32c"])
                    S.op("dve", lambda: V.tensor_reduce(out=s4[:], in_=t32[:].rearrange("p (h d) -> p h d", h=4), axis=AX.X, op=ALU.add),
                         reads=["t32c"], writes=["s4"])
                    S.op("act", lambda: A.activation(out=s4[:], in_=s4[:], func=AF.Ln, scale=1.0 / 128, bias=EPS), reads=["s4"], writes=["s4"])
                    S.op("act", lambda: A.activation(out=s4[:], in_=s4[:], func=AF.Exp, scale=-0.5), reads=["s4"], writes=["s4"])
                    S.op("dve", lambda: V.tensor_tensor(out=ot[:].rearrange("p (h d) -> p h d", h=4), in0=ot[:].rearrange("p (h d) -> p h d", h=4),
                                                        in1=s4[:].unsqueeze(2).to_broadcast([P, 4, P]), op=ALU.mult),
                         reads=["ot", "s4"], writes=["ot"])
                    S.op("dve", lambda: V.tensor_tensor(out=ot[:], in0=ot[:], in1=ognB[:], op=ALU.mult), reads=["ot", "ognB"], writes=["ot"])
                    S.op("dve", lambda: V.tensor_tensor(out=ob[:], in0=ot[:], in1=sgl[:], op=ALU.mult), reads=["ot", "sgl"], writes=["ob"])
                    for h in range(4):
                        S.op("pe", lambda h=h: T.transpose(out=TPb[:, h, :], in_=ob[:, h * P:(h + 1) * P], identity=IDENT),
                             reads=["ob", "cm"], writes=["TPb"], sig=(h == 3))
                    S.op("act", lambda: A.copy(out=obT[:], in_=TPb[:, 0:4, :]), reads=["TPb"], writes=["obT"])
                    for half in range(2):
                        ys = 4 + half
                        for h in range(4):
                            S.op("pe", lambda h=h: T.matmul(PS[ys][:], lhsT=obT[:, h, :], rhs=Wo1[:, h, half * 512:(half + 1) * 512],
                                                            start=(h == 0), stop=(h == 3)),
                                 reads=["obT", "Wo1"], writes=[f"ps{ys}"], sig=(h == 3))
                        hs = H[:, l, half * 512:(half + 1) * 512]
                        S.op("dve", lambda: V.tensor_tensor(out=hs, in0=PS[ys][:], in1=hs, op=ALU.add),
                             reads=[f"ps{ys}", f"H{l}", "H"], writes=[f"H{l}"])

                def rl_of(g):
                    return (0, 0) if g == 0 else ((g - 1) % 8, 1 + (g - 1) // 8)

                def load_B(g):
                    r_, l_ = rl_of(g)
                    S.dma("sp", Bl[g % 3][:].bitcast(BF16), comb_v[g][:, 0:1032], reads=["comb_g"], writes=[f"Bl{g % 3}"], key=f"bl{g % 3}")

                load_B(0)
                load_B(1)
                for g in range(129):
                    r, l = rl_of(g)
                    sl = g % 3
                    sel = Ssel[l % 2]
                    seln = f"Ssel{l % 2}"
                    if g >= 1:
                        if r == 0:
                            S.op("dve", lambda: V.tensor_scalar_mul(out=sel[:], in0=Sst[:], scalar1=pv[:, 2:3]), reads=["Sst", "pv"], writes=[seln])
                        else:
                            S.op("dve", lambda: V.scalar_tensor_tensor(out=sel[:], in0=Sst[:], scalar=pv[:, 2 + r:3 + r], in1=sel[:],
                                                                       op0=ALU.mult, op1=ALU.add),
                                 reads=["Sst", "pv", seln], writes=[seln])
                    if g + 2 < 128:
                        load_B(g + 2)
                    if g < 128:
                        S.op("dve", lambda: V.tensor_tensor(out=Sst[:].rearrange("p (h d) -> p h d", h=4), in0=Sst[:].rearrange("p (h d) -> p h d", h=4),
                                                            in1=Bl[sl][:, 512:516].unsqueeze(2).to_broadcast([P, 4, P]), op=ALU.mult),
                             reads=["Sst", f"Bl{sl}"], writes=["Sst"])
                        S.op("dve", lambda: V.tensor_tensor(out=Sst[:], in0=Sst[:], in1=Bl[sl][:, 0:512], op=ALU.add),
                             reads=["Sst", f"Bl{sl}"], writes=["Sst"])
                    if g >= 1 and r == 7:
                        block_out(l, sel, seln)
                S.barrier()
            if "h2a" in dbg_d:
                dbg_dump("h2a", H[:], ["H"] + [f"H{b}" for b in range(NBL)])
            if dbg.get("_stop") == 2:
                S.barrier()
                S.final_wait(["dbg"])
                return nc

            S.barrier()
            cA.close()
            pstk.close()
            pstk = ExitStack()
            Zp = [pstk.enter_context(nc.psum_tensor(f"psz{i}", [P, 2, 512], F32)) for i in range(4)]
            Ap = Zp
            with ExitStack() as c2:
                osbT = sbt(c2, "osbT", [P, 4, NQ * P], BF16)
                c3 = ExitStack()
                pm = sbt(c3, "pm", [P, 16, P], BF16)
                S.dma("sp", pm[:], pmask_d.rearrange("p (k c) -> p k c", k=16), writes=["pm"], key="init")
                OutAcc = [sbt(c3, f"OutAcc{i}", [P, NQ * P], F32) for i in range(2)]
                Lsum = [sbt(c3, f"Lsum{i}", [P, 2, NQ * P], BF16) for i in range(2)]
                Ks = [[sbt(c3, f"Ks{si}_{i}", [P, 8, P], BF16) for i in range(2)] for si in range(2)]
                Vs = [[sbt(c3, f"Vs{si}_{i}", [P, 8, P], BF16) for i in range(2)] for si in range(2)]
                NSL = 4
                e_t = [sbt(c3, f"e_t{i}", [P, 2, 512], F32) for i in range(NSL)]
                lk_t = [sbt(c3, f"lk_t{i}", [P, 2, 512], BF16) for i in range(NSL)]
                w_t = [sbt(c3, f"w_t{i}", [P, 2, 512], BF16) for i in range(NSL)]
                kvg_v = comb_g.rearrange("(g p) c -> p g c", p=P)[:, :, 1032:2056]
                PR = [slice(0, 64), slice(64, 128)]
                gcount = [0, 0]
                for hp0 in (0, 2):
                    ulist = [[], []]
                    groups = [[], []]
                    for si in range(2):
                        S.op("dve", lambda: V.memset(OutAcc[si][:], 0.0), writes=[f"OA{si}_{q}" for q in range(4)])
                        S.op("dve", lambda: V.memset(Lsum[si][:], 0.0), writes=[f"LS{si}_{q}" for q in range(4)])
                        for Gi in list(range(15, -1, -1)) + [-1]:
                            sl = gcount[si] % 2
                            gcount[si] += 1
                            rs = list(range(7, -1, -1)) if Gi >= 0 else [0]
                            first = len(ulist[si])
                            for r in rs:
                                for q in range(4):
                                    j0 = max(Gi, 4 * q) if Gi >= 0 else 4 * q
                                    if j0 > 4 * q + 3:
                                        continue
                                    ulist[si].append(dict(si=si, hp=hp0 + si, sl=sl, r=r, q=q, j0=j0, j1=4 * q + 4, meta=(Gi < 0),
                                                          bnd=(Gi >= 0 and j0 == Gi)))
                            groups[si].append((Gi, sl, 2 * first + si, 2 * (len(ulist[si]) - 1) + si))
                    units = [u for pair in zip(ulist[0], ulist[1]) for u in pair]

                    def load_group(si, gi):
                        Gi, sl, _, _ = groups[si][gi]
                        hp = hp0 + si
                        kn = f"Ks{si}_{sl}"
                        if Gi >= 0:
                            S.dma("sp", Ks[si][sl][:], kvg_v[:, 8 * Gi + 1:8 * Gi + 9, hp * P:(hp + 1) * P], reads=["comb_g"], writes=[kn], key=kn)
                            S.dma("sp", Vs[si][sl][:], kvg_v[:, 8 * Gi + 1:8 * Gi + 9, (4 + hp) * P:(5 + hp) * P], reads=["comb_g"], writes=[kn + "v"], key=kn)
                        else:
                            S.dma("sp", Ks[si][sl][:, 0, :], kvg_v[:, 0, hp * P:(hp + 1) * P], reads=["comb_g"], writes=[kn], key=kn)
                            S.dma("sp", Vs[si][sl][:, 0, :], kvg_v[:, 0, (4 + hp) * P:(5 + hp) * P], reads=["comb_g"], writes=[kn + "v"], key=kn)

                    def st1(u, i):
                        s = i % NSL
                        si, hp, sl, r = u["si"], u["hp"], u["sl"], u["r"]
                        c0, c1 = u["j0"] * P, u["j1"] * P
                        n = c1 - c0
                        u["n"], u["c0"], u["c1"], u["s"] = n, c0, c1, s
                        kn = f"Ks{si}_{sl}"
                        for hh in range(2):
                            pr = PR[hh]
                            S.op("pe", lambda: T.matmul(Zp[s][:, hh, 0:n], lhsT=Ks[si][sl][pr, r, :], rhs=QT[pr, hp, c0:c1], start=True, stop=False,
                                                        skip_group_check=True),
                                 reads=[kn, "QT"], writes=[f"psz{s}"], sig=(hh == 1))
                        S.op("act", lambda: A.activation(out=e_t[s][:, :, 0:n], in_=Zp[s][:, :, 0:n], func=AF.Exp), reads=[f"psz{s}"], writes=[f"e_t{s}"])
                        S.op("act", lambda: A.activation(out=lk_t[s][:, :, 0:n], in_=e_t[s][:, :, 0:n], func=AF.Ln, bias=1.0),
                             reads=[f"e_t{s}"], writes=[f"lk_t{s}"])
                        if u["bnd"]:
                            S.op("dve", lambda: V.tensor_tensor(out=lk_t[s][:, :, 0:P], in0=lk_t[s][:, :, 0:P],
                                                                in1=pm[:, r, :].unsqueeze(1).to_broadcast([P, 2, P]), op=ALU.mult),
                                 reads=[f"lk_t{s}", "pm"], writes=[f"lk_t{s}"])
                        if u["meta"]:
                            S.op("dve", lambda: V.tensor_scalar_mul(out=lk_t[s][:, :, 0:n], in0=lk_t[s][:, :, 0:n], scalar1=pv[:, 0:1]),
                                 reads=[f"lk_t{s}", "pv"], writes=[f"lk_t{s}"])

                    def st2(u):
                        s, n, c0, c1, sl, r, q, si = u["s"], u["n"], u["c0"], u["c1"], u["sl"], u["r"], u["q"], u["si"]
                        lsn = f"LS{si}_{q}"
                        for hh in range(2):
                            S.op("pe", lambda: T.matmul(Zp[s][:, hh, 0:n], lhsT=NUI, rhs=lk_t[s][:, hh, 0:n], start=False, stop=False,
                                                        skip_group_check=True),
                                 reads=["cm", f"lk_t{s}"], writes=[f"psz{s}"], sig=False)
                            S.op("pe", lambda: T.matmul(Zp[s][:, hh, 0:n], lhsT=NONES, rhs=Lsum[si][:, hh, c0:c1], start=False, stop=(not u["bnd"]),
                                                        skip_group_check=True),
                                 reads=["cm", lsn], writes=[f"psz{s}"], sig=(hh == 1 and not u["bnd"]))
                            if u["bnd"]:
                                S.op("pe", lambda: T.matmul(Zp[s][:, hh, 0:P], lhsT=IDENT, rhs=pm[:, 8 + r, :], start=False, stop=True,
                                                            skip_group_check=True),
                                     reads=["cm", "pm"], writes=[f"psz{s}"], sig=(hh == 1))
                        S.op("dve", lambda: V.tensor_tensor(out=Lsum[si][:, :, c0:c1], in0=Lsum[si][:, :, c0:c1], in1=lk_t[s][:, :, 0:n], op=ALU.add),
                             reads=[lsn, f"lk_t{s}"], writes=[lsn])
                        if u["meta"]:
                            S.op("act", lambda: A.activation(out=w_t[s][:, :, 0:n], in_=Zp[s][:, :, 0:n], func=AF.Exp, bias=pv[:, 1:2]),
                                 reads=[f"psz{s}", "pv"], writes=[f"w_t{s}"])
                        else:
                            S.op("act", lambda: A.activation(out=w_t[s][:, :, 0:n], in_=Zp[s][:, :, 0:n], func=AF.Exp),
                                 reads=[f"psz{s}"], writes=[f"w_t{s}"])

                    def st3(u):
                        s, n, c0, c1, sl, r, q, si = u["s"], u["n"], u["c0"], u["c1"], u["sl"], u["r"], u["q"], u["si"]
                        oan = f"OA{si}_{q}"
                        for hh in range(2):
                            S.op("pe", lambda: T.matmul(Zp[s][:, hh, 0:n], lhsT=Vs[si][sl][:, r, :], rhs=w_t[s][:, hh, 0:n], start=True, stop=True),
                                 reads=[f"Ks{si}_{sl}v", f"w_t{s}"], writes=[f"psz{s}"], sig=(hh == 1))
                        for hh in range(2):
                            pr = PR[hh]
                            S.op("dve", lambda: V.tensor_tensor(out=OutAcc[si][pr, c0:c1], in0=OutAcc[si][pr, c0:c1], in1=Zp[s][pr, hh, 0:n], op=ALU.add),
                                 reads=[oan, f"psz{s}"], writes=[oan])

                    nu = len(units)
                    nxt = [2, 2]
                    for si in range(2):
                        load_group(si, 0)
                        load_group(si, 1)
                    for i in range(nu + 2):
                        if i < nu:
                            st1(units[i], i)
                        if 0 <= i - 1 < nu:
                            st2(units[i - 1])
                        if 0 <= i - 2 < nu:
                            st3(units[i - 2])
                        for si in range(2):
                            if nxt[si] < len(groups[si]) and i - 2 >= groups[si][nxt[si] - 2][3]:
                                load_group(si, nxt[si])
                                nxt[si] += 1
                    assert nxt == [len(groups[0]), len(groups[1])]
                    for si in range(2):
                        S.op("act", lambda: A.copy(out=osbT[:, hp0 + si, :], in_=OutAcc[si][:]), reads=[f"OA{si}_{q}" for q in range(4)], writes=["osbT"])

                S.barrier()
                c3.close()
                Wo2 = sbt(c2, "Wo2", [P, 4, D], BF16)
                S.dma("sp", Wo2[:], wsc["wout"][512:1024, :].rearrange("(h p) n -> p h n", p=P), reads=["wsc_wout"], writes=["Wo2"], key="wo")
                yc = 0
                for l in range(1, NBL):
                    for half in range(2):
                        ys = yc % 2
                        yc += 1
                        yb = Zp[ys][:, 0, :]
                        for hp in range(4):
                            S.op("pe", lambda hp=hp: T.matmul(yb, lhsT=osbT[:, hp, (l - 1) * P:l * P], rhs=Wo2[:, hp, half * 512:(half + 1) * 512],
                                                              start=(hp == 0), stop=(hp == 3)),
                                 reads=["osbT", "Wo2"], writes=[f"psz{ys}"], sig=(hp == 3))
                        hs = H[:, l, half * 512:(half + 1) * 512]
                        S.op("dve", lambda: V.tensor_tensor(out=hs, in0=yb, in1=hs, op=ALU.add),
                             reads=[f"psz{ys}", f"H{l}", "H"], writes=[f"H{l}"])
                S.barrier()
            pstk.close()
            pstk = ExitStack()
            PS = [pstk.enter_context(nc.psum_tensor(f"ps{i}c", [P, 512], F32)) for i in range(7)]
            TPb = pstk.enter_context(nc.psum_tensor("tpbc", [P, 8, P], BF16))
        if "h2" in dbg_d:
            dbg_dump("h2", H[:], ["H"] + [f"H{b}" for b in range(NBL)])
        if dbg.get("_stop") == 3:
            S.barrier()
            S.final_wait(["dbg"])
            return nc

        ffn([list(range(1, 9)), list(range(9, 17))], f2n_d, "f2wi", "f2wo", out_d=y_d)
        S.barrier()
        S.final_wait(["out"] + (["dbg"] if dbg_d else []))
        pstk.close()
    return nc


def _const_mats():
    s = np.arange(P)[:, None]
    t = np.arange(P)[None, :]
    ident = (s == t).astype(np.float32)
    tri = (s <= t).astype(np.float32)
    sel63 = np.broadcast_to((s <= 63), (P, P)).astype(np.float32)
    d1 = tri - sel63
    d4 = 1.0 - tri
    nui = -(s >= t).astype(np.float32)
    nones = -np.ones((P, P), np.float32)
    d1a = tri - np.broadcast_to((s <= 31), (P, P)).astype(np.float32)
    d1b = tri - np.broadcast_to((s <= 95), (P, P)).astype(np.float32)
    return np.concatenate([ident, tri, d1, d4, nui, nones, d1a, d1b], axis=1).astype(ml_dtypes.bfloat16)


def _core_masks(c):
    s = np.arange(P)[:, None]
    t = np.arange(P)[None, :]
    m01 = []
    for r in range(8):
        if r < c:
            m = np.ones((P, P), np.float32)
        elif r == c:
            m = (s < t).astype(np.float32)
        else:
            m = np.zeros((P, P), np.float32)
        m01.append(m)
    negm = [NEG * (1.0 - m) for m in m01]
    pmask = np.concatenate(m01 + negm, axis=1).astype(ml_dtypes.bfloat16)
    pvec = np.zeros((P, 16), np.float32)
    pvec[:, 0] = (np.arange(P) >= 112).astype(np.float32)
    pvec[:, 1] = NEG * (np.arange(P) < 112).astype(np.float32)
    for r in range(8):
        pvec[:, 2 + r] = 1.0 if r == c else 0.0
    pvec[:, 10] = 1.0
    pvec[:, 11] = NEG * (np.arange(P) >= 64)
    pvec[:, 12] = NEG * (np.arange(P) < 64)
    return pmask, pvec


def make_in_maps(inputs):
    f32 = lambda a: np.ascontiguousarray(np.asarray(a, dtype=np.float32))
    x = f32(inputs["x"])[0]
    meta = f32(inputs["meta_tokens"])
    blk0 = np.concatenate([np.zeros((P - 16, D), np.float32), meta], axis=0)
    xb = x.reshape(128, P, D)
    cmat = _const_mats()
    shared = {
        "ffn1_norm": f32(inputs["ffn1_norm"]).reshape(1, D),
        "ffn1_w_in": f32(inputs["ffn1_w_in"])[0],
        "ffn1_w_out": f32(inputs["ffn1_w_out"])[0],
        "mix_norm": f32(inputs["mix_norm"]).reshape(1, D),
        "w_in": f32(inputs["w_in"])[0],
        "hgrn_lb_logits": f32(inputs["hgrn_lb_logits"]).reshape(1, 1024),
        "hgrn_out_norm": f32(inputs["hgrn_out_norm"]).reshape(1, 512),
        "sb_q_norm": f32(inputs["sb_q_norm"]).reshape(1, 64),
        "sb_k_norm": f32(inputs["sb_k_norm"]).reshape(1, 64),
        "w_out": f32(inputs["w_out"])[0],
        "ffn2_norm": f32(inputs["ffn2_norm"]).reshape(1, D),
        "ffn2_w_in": f32(inputs["ffn2_w_in"])[0],
        "ffn2_w_out": f32(inputs["ffn2_w_out"])[0],
        "cmat": cmat,
    }
    xall = np.ascontiguousarray(np.concatenate([blk0[None], xb], axis=0))
    shared["xall"] = xall
    in_maps = []
    for c in range(NCORES):
        xc = np.concatenate([blk0[None], xb[c::8]], axis=0)
        pmask, pvec = _core_masks(c)
        m = dict(shared)
        m["x"] = np.ascontiguousarray(xc)
        m["pmask"] = pmask
        m["pvec"] = pvec
        in_maps.append(m)
    return in_maps


_NC_CACHE = {}


def kernel(**inputs):
    if "nc" not in _NC_CACHE:
        _NC_CACHE["nc"] = build_program()
    nc = _NC_CACHE["nc"]
    in_maps = make_in_maps(inputs)
    res = run_bass_kernel_spmd(nc, in_maps, core_ids=list(range(NCORES)))
    out = np.zeros((128, P, D), np.float32)
    for c in range(NCORES):
        out[c::8] = res.results[c]["y"]
    return out.reshape(1, 128 * P, D)
```

```python
import numpy as np
import ml_dtypes
from contextlib import ExitStack

import concourse.bass as bass
import concourse.mybir as mybir
from concourse.bass_utils import run_bass_kernel_spmd

F32 = mybir.dt.float32
BF16 = mybir.dt.bfloat16
AF = mybir.ActivationFunctionType
ALU = mybir.AluOpType
AX = mybir.AxisListType

NCORES = 8
P = 128
D = 1024
DFF = 2816
NF = DFF // P
NBL = 17
NQ = 16
INC = 3584
EPS = 1e-6
NEG = -30000.0


class _Tok:
    __slots__ = ("sem", "val", "eng")

    def __init__(self, sem, val, eng):
        self.sem, self.val, self.eng = sem, val, eng


class _Buf:
    __slots__ = ("name", "w", "r")

    def __init__(self, name):
        self.name, self.w, self.r = name, None, []


class _Eng:
    def __init__(self, name, h, sem):
        self.name, self.h, self.sem = name, h, sem
        self.cnt = 0
        self.pend = None
        self.waited = {}


class Sched:
    def __init__(self, nc, es):
        self.nc, self.es = nc, es
        mk = lambda n: es.enter_context(nc.semaphore(n))
        self.E = {
            "pe": _Eng("pe", nc.tensor, mk("s_pe")),
            "act": _Eng("act", nc.scalar, mk("s_act")),
            "dve": _Eng("dve", nc.vector, mk("s_dve")),
            "pool": _Eng("pool", nc.gpsimd, mk("s_pool")),
            "sp": _Eng("sp", nc.sync, mk("s_sp")),
        }
        self.dsem = {}
        self.dsem_by_sem = {}
        self.bufs = {}
        self.ccsem = mk("s_cc")
        self.ccn = 0
        self.nops = 0

    def B(self, name):
        if name not in self.bufs:
            self.bufs[name] = _Buf(name)
        return self.bufs[name]

    def _need(self, e, tok, need):
        if tok is None:
            return
        if tok.eng is e and e.name == "pe":
            return
        if tok.val is None:
            raise RuntimeError(f"dependency on unsignalled op of {tok.eng.name} from {e.name}")
        k = id(tok.sem)
        val = tok.val
        if tok.eng is None:
            val = self.dsem_by_sem[k][1]
        if need.get(k, (None, 0))[1] < val:
            need[k] = (tok.sem, val)

    def _collect(self, e, reads, writes, skip_sem=None):
        need = {}
        for b in reads:
            b = self.B(b)
            if b.w is not None and b.w.sem is not skip_sem:
                self._need(e, b.w, need)
        for b in writes:
            b = self.B(b)
            if b.w is not None and b.w.sem is not skip_sem:
                self._need(e, b.w, need)
            for t in b.r:
                self._need(e, t, need)
        for k, (sem, val) in need.items():
            if e.waited.get(k, 0) < val:
                e.h.wait_ge(sem, val)
                e.waited[k] = val

    def _record(self, tok, reads, writes):
        for b in reads:
            b = self.B(b)
            b.r = [t for t in b.r if not (t.sem is tok.sem and t is not tok and t.val is not None
                                          and tok.val is not None and t.val <= tok.val)]
            if tok not in b.r:
                b.r.append(tok)
        for b in writes:
            b = self.B(b)
            b.w = tok
            b.r = []

    def op(self, en, fn, reads=(), writes=(), sig=True):
        e = self.E[en]
        ex = [b for b in reads if b.startswith("ps") or b == "TPb"]
        if ex:
            reads = [b for b in reads if b not in ex]
            writes = list(writes) + ex
        self._collect(e, reads, writes)
        ins = fn()
        self.nops += 1
        if e.pend is None:
            e.pend = _Tok(e.sem, None, e)
        tok = e.pend
        self._record(tok, reads, writes)
        if sig:
            e.cnt += 1
            ins.then_inc(e.sem, 1)
            tok.val = e.cnt
            e.pend = None
        return ins

    def dma(self, qn, out, in_, reads=(), writes=(), key="d"):
        e = self.E[qn]
        if key not in self.dsem:
            self.dsem[key] = [self.es.enter_context(self.nc.semaphore("sd_" + key)), 0]
            self.dsem_by_sem[id(self.dsem[key][0])] = self.dsem[key]
        ds = self.dsem[key]
        self._collect(e, reads, writes, skip_sem=ds[0])
        ins = e.h.dma_start(out=out, in_=in_)
        ds[1] += 16
        ins.then_inc(ds[0], 16)
        tok = _Tok(ds[0], ds[1], None)
        self._record(tok, reads, writes)
        self.nops += 1
        return ins

    def allgather(self, in_ap, out_ap, reads, writes, scratch):
        e = self.E["pool"]
        self._collect(e, reads, writes)
        ins = e.h.collective_compute("AllGather", ALU.bypass, replica_groups=[list(range(NCORES))],
                                     ins=[in_ap.opt()], outs=[out_ap.opt()])
        self.ccn += 1
        ins.then_inc(self.ccsem)
        e.h.wait_ge(self.ccsem, self.ccn)
        self.op("pool", lambda: e.h.memset(scratch, 0.0), reads=reads, writes=list(writes) + ["_ccscratch"])

    def barrier(self):
        toks = []
        for e in self.E.values():
            assert e.pend is None, f"pending unsignalled op on {e.name} at barrier"
            if e.cnt:
                toks.append((e.sem, e.cnt, e))
        for key, (sem, cnt) in self.dsem.items():
            if cnt:
                toks.append((sem, cnt, None))
        for e in self.E.values():
            for sem, val, src in toks:
                if src is e:
                    continue
                k = id(sem)
                if e.waited.get(k, 0) < val:
                    e.h.wait_ge(sem, val)
                    e.waited[k] = val

    def final_wait(self, keys):
        e = self.E["sp"]
        for key in keys:
            sem, cnt = self.dsem[key]
            e.h.wait_ge(sem, cnt)


def build_program(dbg=None):
    dbg = dbg or {}
    nc = bass.Bass("TRN2", target_bir_lowering=False)
    dt = nc.dram_tensor
    x_d = dt("x", [NBL, P, D], F32, kind="ExternalInput").ap()
    xall_d = dt("xall", [129, P, D], F32, kind="ExternalInput").ap()
    f1n_d = dt("ffn1_norm", [1, D], F32, kind="ExternalInput").ap()
    f1wi_d = dt("ffn1_w_in", [D, 2 * DFF], F32, kind="ExternalInput").ap()
    f1wo_d = dt("ffn1_w_out", [DFF, D], F32, kind="ExternalInput").ap()
    mixn_d = dt("mix_norm", [1, D], F32, kind="ExternalInput").ap()
    win_d = dt("w_in", [D, INC], F32, kind="ExternalInput").ap()
    lbl_d = dt("hgrn_lb_logits", [1, 1024], F32, kind="ExternalInput").ap()
    ogn_d = dt("hgrn_out_norm", [1, 512], F32, kind="ExternalInput").ap()
    qg_d = dt("sb_q_norm", [1, 64], F32, kind="ExternalInput").ap()
    kg_d = dt("sb_k_norm", [1, 64], F32, kind="ExternalInput").ap()
    wout_d = dt("w_out", [D, D], F32, kind="ExternalInput").ap()
    f2n_d = dt("ffn2_norm", [1, D], F32, kind="ExternalInput").ap()
    f2wi_d = dt("ffn2_w_in", [D, 2 * DFF], F32, kind="ExternalInput").ap()
    f2wo_d = dt("ffn2_w_out", [DFF, D], F32, kind="ExternalInput").ap()
    cmat_d = dt("cmat", [P, 8 * P], BF16, kind="ExternalInput").ap()
    pmask_d = dt("pmask", [P, 16 * P], BF16, kind="ExternalInput").ap()
    pvec_d = dt("pvec", [P, 16], F32, kind="ExternalInput").ap()
    y_d = dt("y", [NQ, P, D], F32, kind="ExternalOutput").ap()
    dbg_d = {}
    for name, shape in dbg.items():
        if name.startswith("_"):
            continue
        if shape[-1] == "bf16":
            dbg_d[name] = dt("dbg_" + name, list(shape[:-1]), BF16, kind="ExternalOutput").ap()
        else:
            dbg_d[name] = dt("dbg_" + name, list(shape), F32, kind="ExternalOutput").ap()

    CW = 2056
    comb_g = dt("comb_g", [129 * P, CW], BF16).ap()
    wsc = {}
    for nm, shp in (("f1wi", [D, 2 * DFF]), ("f1wo", [DFF, D]), ("win", [D, INC]), ("wout", [D, D]),
                    ("f2wi", [D, 2 * DFF]), ("f2wo", [DFF, D])):
        wsc[nm] = dt("wsc_" + nm, shp, BF16).ap()

    es = ExitStack()
    with es:
        S = Sched(nc, es)

        used_names = {}

        def sbt(ctx, name, shape, dtype):
            k = used_names.get(name, 0)
            used_names[name] = k + 1
            return ctx.enter_context(nc.sbuf_tensor(name if k == 0 else f"{name}_v{k}", shape, dtype))

        cm = sbt(es, "cm", [P, 8, P], BF16)
        IDENT, TRI, D1, D4, NUI, NONES, D1A, D1B = (cm[:, i, :] for i in range(8))
        pv = sbt(es, "pv", [P, 16], F32)
        gB = sbt(es, "gB", [P, D], F32)
        lbB = sbt(es, "lbB", [P, 512], F32)
        omlB = sbt(es, "omlB", [P, 512], F32)
        ognB = sbt(es, "ognB", [P, 512], F32)
        qgB = sbt(es, "qgB", [P, 64], F32)
        kgB = sbt(es, "kgB", [P, 64], F32)
        ssqz = sbt(es, "ssqz", [P, 400], F32)
        rst = sbt(es, "rst", [P, 400], F32)
        ccs = sbt(es, "ccs", [P, 8], F32)
        pstk = ExitStack()
        PS = [pstk.enter_context(nc.psum_tensor(f"ps{i}", [P, 512], F32)) for i in range(7)]
        TPb = pstk.enter_context(nc.psum_tensor("tpb", [P, 8, P], BF16))
        ssq_next = [0]

        def new_col():
            c = ssq_next[0]
            ssq_next[0] += 1
            assert c < 400
            return c

        V, A, T, G, SP = nc.vector, nc.scalar, nc.tensor, nc.gpsimd, nc.sync

        S.dma("sp", cm[:], cmat_d.rearrange("p (k c) -> p k c", k=8), writes=["cm"], key="init")
        S.dma("sp", pv[:], pvec_d, writes=["pv"], key="init2")
        S.dma("sp", ognB[:], ogn_d.broadcast_to([P, 512]), writes=["ognB"], key="init2")
        S.dma("sp", qgB[:], qg_d.broadcast_to([P, 64]), writes=["qgB"], key="init2")
        S.dma("sp", kgB[:], kg_d.broadcast_to([P, 64]), writes=["kgB"], key="init2")
        S.op("dve", lambda: V.memset(ssqz[:], 0.0), writes=["ssqz"])
        with ExitStack() as c0x:
            lgt = sbt(c0x, "lgt", [P, 1024], F32)
            S.dma("sp", lgt[:], lbl_d.broadcast_to([P, 1024]), writes=["lgt"], key="init2")
            S.op("dve", lambda: V.tensor_tensor(out=lbB[:], in0=lgt[:, 512:1024], in1=lgt[:, 0:512], op=ALU.subtract),
                 reads=["lgt"], writes=["lbB"])
            S.op("act", lambda: A.activation(out=omlB[:], in_=lbB[:], func=AF.Exp), reads=["lbB"], writes=["omlB"])
            S.op("dve", lambda: V.tensor_scalar_add(out=lbB[:], in0=omlB[:], scalar1=1.0), reads=["omlB"], writes=["lbB"])
            S.op("dve", lambda: V.reciprocal(out=lbB[:], in_=lbB[:]), reads=["lbB"], writes=["lbB"])
            S.op("dve", lambda: V.tensor_scalar(out=omlB[:], in0=lbB[:], scalar1=-1.0, scalar2=1.0, op0=ALU.mult, op1=ALU.add),
                 reads=["lbB"], writes=["omlB"])
            S.barrier()
        S.op("dve", lambda: V.tensor_scalar_mul(out=qgB[:], in0=qgB[:], scalar1=0.125), reads=["qgB"], writes=["qgB"])

        for nm, src in (("f1wi", f1wi_d), ("f1wo", f1wo_d), ("win", win_d), ("wout", wout_d), ("f2wi", f2wi_d), ("f2wo", f2wo_d)):
            R, C = src.shape
            RT = 256
            for r0 in range(0, R, RT):
                rn = min(RT, R - r0)
                S.dma("pool", wsc[nm][r0:r0 + rn, :], src[r0:r0 + rn, :], writes=["wsc_" + nm], key="pc_" + nm)

        def rmsnorm_gen(src, srcn, xn, xnn, xnT_dst, tag, gt=None, gtn="gB"):
            gt = gB if gt is None else gt
            c = new_col()
            yield S.op("act", lambda: A.activation(out=xn[:], in_=src, func=AF.Square, accum_out=ssqz[:, c:c + 1]),
                       reads=list(srcn) + ["ssqz"], writes=[xnn, f"ssq{c}"])
            yield S.op("act", lambda: A.activation(out=rst[:, c:c + 1], in_=ssqz[:, c:c + 1], func=AF.Ln, scale=1.0 / D, bias=EPS),
                       reads=[f"ssq{c}"], writes=[f"rst{c}"])
            yield S.op("act", lambda: A.activation(out=rst[:, c:c + 1], in_=rst[:, c:c + 1], func=AF.Exp, scale=-0.5),
                       reads=[f"rst{c}"], writes=[f"rst{c}"])
            yield S.op("dve", lambda: V.scalar_tensor_tensor(out=xn[:], in0=src, scalar=rst[:, c:c + 1], in1=gt[:],
                                                             op0=ALU.mult, op1=ALU.mult),
                       reads=list(srcn) + [f"rst{c}", gtn], writes=[xnn])
            for kc in range(8):
                S.op("pe", lambda kc=kc: T.transpose(out=TPb[:, kc, :], in_=xn[:, kc * P:(kc + 1) * P], identity=IDENT),
                     reads=[xnn, "cm"], writes=["TPb"], sig=(kc == 7))
            yield S.op("act", lambda: A.copy(out=xnT_dst, in_=TPb[:]), reads=["TPb"], writes=[tag])

        def lockstep(gens):
            live = list(gens)
            while live:
                for g in list(live):
                    try:
                        next(g)
                    except StopIteration:
                        live.remove(g)

        def rmsnorm_T(src, srcn, xn, xnn, xnT_dst, tag, gt=None, gtn="gB"):
            lockstep([rmsnorm_gen(src, srcn, xn, xnn, xnT_dst, tag, gt, gtn)])

        def dbg_dump(name, ap_sb, reads):
            if name in dbg_d:
                S.dma("sp", dbg_d[name], ap_sb, reads=reads, key="dbg")

        ffc = {"fc": 0, "gc": 0, "yc": 0}

        def ffn_alloc(cx, width):
            t = {}
            t["W2"] = sbt(cx, "W2", [P, NF, D], BF16)
            t["hid"] = sbt(cx, "hid", [P, NF, width], BF16)
            t["xnT"] = sbt(cx, "xnT", [P, 8, width], BF16)
            t["W1"] = [sbt(cx, f"W1_{i}", [P, 8, 256], BF16) for i in range(3)]
            t["xn"] = [sbt(cx, f"xn{i}", [P, D], BF16) for i in range(2)]
            t["sg"] = [sbt(cx, f"sg{i}", [P, 512], F32) for i in range(2)]
            return t

        def ffn_load_w2(t, wo_name):
            wo_v = wsc[wo_name].rearrange("(f p) n -> p f n", p=P)
            for i in range(0, NF, 6):
                j = min(NF, i + 6)
                S.dma("sp", t["W2"][:, i:j, :], wo_v[:, i:j, :], reads=["wsc_" + wo_name], writes=["W2"], key="w2")

        def ffn_pass(t, pb, Hs, hpfx, wi_name, gt=None, gtn="gB", out_d=None, after_mm1=None):
            W2, hid, xnT, W1, xn, sg = t["W2"], t["hid"], t["xnT"], t["W1"], t["xn"], t["sg"]
            wi_v = wsc[wi_name].rearrange("(kc p) n -> p kc n", p=P)
            Tn = len(pb) * P
            for i0 in range(0, len(pb), 2):
                lockstep([rmsnorm_gen(Hs[:, pb[i], :], [f"{hpfx}{pb[i]}", hpfx], xn[i % 2], f"xn{i % 2}", xnT[:, :, i * P:(i + 1) * P],
                                      "xnT", gt, gtn) for i in range(i0, min(i0 + 2, len(pb)))])
            groups = [(t0, min(512, Tn - t0)) for t0 in range(0, Tn, 512)]
            for f in range(NF):
                sl = ffc["fc"] % 3
                ffc["fc"] += 1
                S.dma("sp", W1[sl][:, :, 0:P], wi_v[:, :, f * P:(f + 1) * P], reads=["wsc_" + wi_name], writes=[f"W1_{sl}"], key=f"w1_{sl}")
                S.dma("sp", W1[sl][:, :, P:2 * P], wi_v[:, :, DFF + f * P:DFF + (f + 1) * P],
                      reads=["wsc_" + wi_name], writes=[f"W1_{sl}"], key=f"w1_{sl}")
                for (t0, tn) in groups:
                    gs = ffc["gc"] % 2
                    ffc["gc"] += 1
                    Gp, Up = PS[gs], PS[2 + gs]
                    for kc in range(8):
                        S.op("pe", lambda kc=kc: T.matmul(Gp[:, 0:tn], lhsT=W1[sl][:, kc, 0:P], rhs=xnT[:, kc, t0:t0 + tn],
                                                          start=(kc == 0), stop=(kc == 7)),
                             reads=[f"W1_{sl}", "xnT"], writes=[f"ps{gs}"], sig=False)
                    for kc in range(8):
                        S.op("pe", lambda kc=kc: T.matmul(Up[:, 0:tn], lhsT=W1[sl][:, kc, P:2 * P], rhs=xnT[:, kc, t0:t0 + tn],
                                                          start=(kc == 0), stop=(kc == 7)),
                             reads=[f"W1_{sl}", "xnT"], writes=[f"ps{2 + gs}"], sig=(kc == 7))
                    S.op("act", lambda: A.activation(out=sg[gs][:, 0:tn], in_=Gp[:, 0:tn], func=AF.Silu),
                         reads=[f"ps{gs}"], writes=[f"sg{gs}"])
                    S.op("dve", lambda: V.tensor_tensor(out=hid[:, f, t0:t0 + tn], in0=sg[gs][:, 0:tn], in1=Up[:, 0:tn],
                                                        op=ALU.mult),
                         reads=[f"sg{gs}", f"ps{2 + gs}"], writes=["hid"])
            if after_mm1 is not None:
                after_mm1()
            for i, b in enumerate(pb):
                for half in range(2):
                    ys = 4 + (ffc["yc"] % 2)
                    ffc["yc"] += 1
                    for f in range(NF):
                        S.op("pe", lambda f=f: T.matmul(PS[ys][:], lhsT=hid[:, f, i * P:(i + 1) * P],
                                                        rhs=W2[:, f, half * 512:(half + 1) * 512],
                                                        start=(f == 0), stop=(f == NF - 1)),
                             reads=["hid", "W2"], writes=[f"ps{ys}"], sig=(f == NF - 1))
                    hs = Hs[:, b, half * 512:(half + 1) * 512]
                    S.op("dve", lambda: V.scalar_tensor_tensor(out=hs, in0=PS[ys][:], scalar=0.5, in1=hs,
                                                               op0=ALU.mult, op1=ALU.add),
                         reads=[f"ps{ys}", f"{hpfx}{b}", hpfx], writes=[f"{hpfx}{b}"])
                if out_d is not None:
                    S.dma("sp", out_d[b - 1], Hs[:, b, :], reads=[f"{hpfx}{b}"], key="out")

        def ffn(passes, gain_d, wi_name, wo_name, out_d=None):
            with ExitStack() as cx:
                t = ffn_alloc(cx, 1024)
                S.dma("sp", gB[:], gain_d.broadcast_to([P, D]), writes=["gB"], key="gain")
                ffn_load_w2(t, wo_name)
                for pb in passes:
                    ffn_pass(t, pb, H, "H", wi_name, out_d=out_d)
                S.barrier()

        comb_v = comb_g.rearrange("(g p) c -> g p c", p=P)
        with ExitStack() as cx:
            t = ffn_alloc(cx, 512)
            Ht = sbt(cx, "Ht", [P, 4, D], F32)
            gBm = sbt(cx, "gBm", [P, D], F32)
            Wf = sbt(cx, "Wf", [P, 8, 2048], BF16)
            win_v0 = wsc["win"].rearrange("(kc p) n -> p kc n", p=P)
            S.dma("sp", gB[:], f1n_d.broadcast_to([P, D]), writes=["gB"], key="gain")
            S.dma("sp", gBm[:], mixn_d.broadcast_to([P, D]), writes=["gBm"], key="gain")
            def late_loads():
                ffn_load_w2(t, "f1wo")
                for i, c0 in enumerate((512, 1024, 2560, 3072)):
                    S.dma("sp", Wf[:, :, i * 512:(i + 1) * 512], win_v0[:, :, c0:c0 + 512], reads=["wsc_win"], writes=["Wf"], key="win")
            xnm = t["xn"]
            TS = []
            for k in range(2):
                d = {}
                d["xnT1"] = sbt(cx, f"xnT1f{k}", [P, 8, P], BF16)
                for nm in ("sgm", "ff", "kf", "et", "t32"):
                    d[nm] = sbt(cx, f"{nm}F{k}", [P, 512], F32)
                for nm in ("lfh", "lfl", "kpp", "vb", "nrm"):
                    d[nm] = sbt(cx, f"{nm}F{k}", [P, 512], BF16)
                d["bst"] = sbt(cx, f"bstF{k}", [P, 516], F32)
                d["kvst"] = sbt(cx, f"kvstF{k}", [P, 8, P], BF16)
                d["s8"] = sbt(cx, f"s8F{k}", [P, 8], F32)
                TS.append(d)
            onec = sbt(cx, "onecF", [P, 1], BF16)
            S.op("dve", lambda: V.memset(onec[:], 1.0), writes=["onecF"])

            def p2lite(i, gb, k):
                d = TS[k]
                xnT1, sgm, ff, kf, et, t32 = d["xnT1"], d["sgm"], d["ff"], d["kf"], d["et"], d["t32"]
                lfh, lfl, kpp, vb, nrm, bs, kv, s8 = d["lfh"], d["lfl"], d["kpp"], d["vb"], d["nrm"], d["bst"], d["kvst"], d["s8"]
                N = lambda nm: f"{nm}F{k}"
                QA, QB, QC = PS[3 * k], PS[3 * k + 1], PS[3 * k + 2]
                qa, qb, qc = f"ps{3 * k}", f"ps{3 * k + 1}", f"ps{3 * k + 2}"

                def projF(dst, dname, c0):
                    for kc in range(8):
                        yield S.op("pe", lambda kc=kc: T.matmul(dst[:], lhsT=xnT1[:, kc, :], rhs=Wf[:, kc, c0:c0 + 512],
                                                                start=(kc == 0), stop=(kc == 7)),
                                   reads=[N("xnT1"), "Wf"], writes=[dname], sig=(kc == 7))

                c = new_col()
                src = Ht[:, i, :]
                yield S.op("act", lambda: A.activation(out=xnm[k][:], in_=src, func=AF.Square, accum_out=ssqz[:, c:c + 1]),
                           reads=[f"Ht{i}", "Ht", "ssqz"], writes=[f"xn{k}", f"ssq{c}"])
                yield S.op("act", lambda: A.activation(out=rst[:, c:c + 1], in_=ssqz[:, c:c + 1], func=AF.Ln, scale=1.0 / D, bias=EPS),
                           reads=[f"ssq{c}"], writes=[f"rst{c}"])
                yield S.op("act", lambda: A.activation(out=rst[:, c:c + 1], in_=rst[:, c:c + 1], func=AF.Exp, scale=-0.5),
                           reads=[f"rst{c}"], writes=[f"rst{c}"])
                yield S.op("dve", lambda: V.scalar_tensor_tensor(out=xnm[k][:], in0=src, scalar=rst[:, c:c + 1], in1=gBm[:],
                                                                 op0=ALU.mult, op1=ALU.mult),
                           reads=[f"Ht{i}", "Ht", f"rst{c}", "gBm"], writes=[f"xn{k}"])
                for kc in range(8):
                    S.op("pe", lambda kc=kc: T.transpose(out=TPb[:, kc, :], in_=xnm[k][:, kc * P:(kc + 1) * P], identity=IDENT),
                         reads=[f"xn{k}", "cm"], writes=["TPb"], sig=(kc == 7))
                yield S.op("act", lambda: A.copy(out=xnT1[:], in_=TPb[:]), reads=["TPb"], writes=[N("xnT1")])
                yield from projF(QA, qa, 0)
                yield from projF(QB, qb, 512)
                yield S.op("act", lambda: A.activation(out=sgm[:], in_=QA[:], func=AF.Sigmoid), reads=[qa], writes=[N("sgm")])
                yield S.op("dve", lambda: V.tensor_tensor(out=ff[:], in0=sgm[:], in1=omlB[:], op=ALU.mult), reads=[N("sgm"), "omlB"], writes=[N("ff")])
                yield S.op("dve", lambda: V.tensor_tensor(out=ff[:], in0=ff[:], in1=lbB[:], op=ALU.add), reads=[N("ff"), "lbB"], writes=[N("ff")])
                yield S.op("act", lambda: A.activation(out=ff[:], in_=ff[:], func=AF.Ln), reads=[N("ff")], writes=[N("ff")])
                yield S.op("dve", lambda: V.tensor_scalar(out=kf[:], in0=sgm[:], scalar1=-1.0, scalar2=1.0, op0=ALU.mult, op1=ALU.add),
                           reads=[N("sgm")], writes=[N("kf")])
                yield S.op("dve", lambda: V.tensor_tensor(out=kf[:], in0=kf[:], in1=omlB[:], op=ALU.mult), reads=[N("kf"), "omlB"], writes=[N("kf")])
                if gb == 0:
                    yield S.op("dve", lambda: V.tensor_scalar_mul(out=ff[:], in0=ff[:], scalar1=pv[:, 0:1]), reads=[N("ff"), "pv"], writes=[N("ff")])
                    yield S.op("dve", lambda: V.tensor_scalar_mul(out=kf[:], in0=kf[:], scalar1=pv[:, 0:1]), reads=[N("kf"), "pv"], writes=[N("kf")])
                yield S.op("dve", lambda: V.tensor_copy(out=lfh[:], in_=ff[:]), reads=[N("ff")], writes=[N("lfh")])
                yield S.op("dve", lambda: V.tensor_tensor(out=lfl[:], in0=ff[:], in1=lfh[:], op=ALU.subtract), reads=[N("ff"), N("lfh")], writes=[N("lfl")])
                S.op("pe", lambda: T.matmul(QC[:], lhsT=D4, rhs=lfh[:], start=True, stop=False), reads=["cm", N("lfh")], writes=[qc], sig=False)
                yield S.op("pe", lambda: T.matmul(QC[:], lhsT=D4, rhs=lfl[:], start=False, stop=True), reads=["cm", N("lfl")], writes=[qc], sig=True)
                for h in range(4):
                    S.op("pe", lambda h=h: T.matmul(QA[:, h:h + 1], lhsT=lfh[:, h * P:(h + 1) * P], rhs=onec[:], start=True, stop=False),
                         reads=[N("lfh"), "onecF"], writes=[qa], sig=False)
                    S.op("pe", lambda h=h: T.matmul(QA[:, h:h + 1], lhsT=lfl[:, h * P:(h + 1) * P], rhs=onec[:], start=False, stop=True),
                         reads=[N("lfl"), "onecF"], writes=[qa], sig=(h == 3))
                yield
                yield S.op("act", lambda: A.activation(out=bs[:, 512:516], in_=QA[:, 0:4], func=AF.Exp), reads=[qa], writes=[N("bst")])
                yield S.op("act", lambda: A.activation(out=et[:], in_=QC[:], func=AF.Exp), reads=[qc], writes=[N("et")])
                yield S.op("dve", lambda: V.tensor_tensor(out=kpp[:], in0=kf[:], in1=et[:], op=ALU.mult), reads=[N("kf"), N("et")], writes=[N("kpp")])
                yield S.op("act", lambda: A.copy(out=vb[:], in_=QB[:]), reads=[qb], writes=[N("vb")])
                for h in range(4):
                    S.op("pe", lambda h=h: T.matmul(QC[:, h * P:(h + 1) * P], lhsT=kpp[:, h * P:(h + 1) * P], rhs=vb[:, h * P:(h + 1) * P],
                                                    start=True, stop=True),
                         reads=[N("kpp"), N("vb")], writes=[qc], sig=(h == 3))
                yield
                yield S.op("dve", lambda: V.tensor_copy(out=bs[:, 0:512], in_=QC[:]), reads=[qc], writes=[N("bst")])
                yield S.dma("sp", comb_v[gb][:, 0:1032], bs[:].bitcast(BF16), reads=[N("bst")], writes=["comb_g"], key=f"bastF{k}")
                yield from projF(QA, qa, 1024)
                yield from projF(QB, qb, 1536)
                yield S.op("act", lambda: A.activation(out=t32[:], in_=QA[:], func=AF.Square), reads=[qa], writes=[N("t32")])
                yield S.op("dve", lambda: V.tensor_reduce(out=s8[:], in_=t32[:].rearrange("p (h d) -> p h d", h=8), axis=AX.X, op=ALU.add),
                           reads=[N("t32")], writes=[N("s8")])
                yield S.op("act", lambda: A.activation(out=s8[:], in_=s8[:], func=AF.Ln, scale=1.0 / 64, bias=EPS), reads=[N("s8")], writes=[N("s8")])
                yield S.op("act", lambda: A.activation(out=s8[:], in_=s8[:], func=AF.Exp, scale=-0.5), reads=[N("s8")], writes=[N("s8")])
                yield S.op("dve", lambda: V.tensor_tensor(out=t32[:].rearrange("p (h d) -> p h d", h=8),
                                                          in0=QA[:].rearrange("p (h d) -> p h d", h=8),
                                                          in1=s8[:].unsqueeze(2).to_broadcast([P, 8, 64]), op=ALU.mult),
                           reads=[qa, N("s8")], writes=[N("t32")])
                yield S.op("dve", lambda: V.tensor_tensor(out=nrm[:].rearrange("p (h d) -> p h d", h=8),
                                                          in0=t32[:].rearrange("p (h d) -> p h d", h=8),
                                                          in1=kgB[:].unsqueeze(1).to_broadcast([P, 8, 64]), op=ALU.mult),
                           reads=[N("t32"), "kgB"], writes=[N("nrm")])
                yield S.op("act", lambda: A.copy(out=kv[:, 4:8, :], in_=QB[:].rearrange("p (a c) -> p a c", a=4)), reads=[qb], writes=[N("kvst")])
                for hp in range(4):
                    S.op("pe", lambda hp=hp: T.transpose(out=TPb[:, hp, :], in_=nrm[:, hp * P:(hp + 1) * P], identity=IDENT),
                         reads=[N("nrm"), "cm"], writes=["TPb"], sig=(hp == 3))
                yield S.op("dve", lambda: V.tensor_copy(out=kv[:, 0:4, :], in_=TPb[:, 0:4, :]), reads=["TPb"], writes=[N("kvst")])
                yield S.dma("sp", comb_v[gb][:, 1032:2056].rearrange("p (a c) -> p a c", a=8), kv[:], reads=[N("kvst")], writes=["comb_g"],
                            key=f"kvstF{k}")

            nsp = (129 + 3) // 4

            def load_ht(sp, i):
                gb = 4 * sp + i
                if sp < nsp and gb < 129:
                    S.dma("sp", Ht[:, i, :], xall_d[gb], writes=[f"Ht{i}"], key=f"xt{i}")

            for i in range(4):
                load_ht(0, i)
            for sp in range(nsp):
                gbs = list(range(4 * sp, min(4 * sp + 4, 129)))
                n = len(gbs)
                ffn_pass(t, list(range(n)), Ht, "Ht", "f1wi", after_mm1=(late_loads if sp == 0 else None))
                for i0 in range(0, 4, 2):
                    if i0 < n:
                        lockstep([p2lite(i, gbs[i], i - i0) for i in range(i0, min(i0 + 2, n))])
                    load_ht(sp + 1, i0)
                    load_ht(sp + 1, i0 + 1)
            S.barrier()

        H = sbt(es, "H", [P, NBL, D], F32)
        S.dma("sp", H[:], x_d.rearrange("b p d -> p b d"), writes=["H"], key="x")

        skip = dbg.get("_skip", ())
        if "ffn1" not in skip:
            ffn([list(range(1, 9)), list(range(9, 17))], f1n_d, "f1wi", "f1wo")
        if "h1" in dbg_d and dbg.get("_stop") == 1:
            dbg_dump("h1", H[:], ["H"] + [f"H{b}" for b in range(NBL)])

        if dbg.get("_stop") == 1:
            S.barrier()
            S.final_wait(["dbg"])
            return nc

        with ExitStack() as c1:
            QT = sbt(c1, "QT", [P, 4, NQ * P], BF16)
            cA = ExitStack()
            qppT = sbt(cA, "qppT", [P, 4, NQ * P], BF16)
            oin = sbt(cA, "oin", [P, NQ, 512], BF16)
            S.dma("sp", gB[:], mixn_d.broadcast_to([P, D]), writes=["gB"], key="gain")
            win_v = wsc["win"].rearrange("(kc p) n -> p kc n", p=P)

            with ExitStack() as c2:
                Wh = sbt(c2, "Wh", [P, 8, 1536], BF16)
                for i in range(3):
                    S.dma("sp", Wh[:, :, i * 512:(i + 1) * 512], win_v[:, :, i * 512:(i + 1) * 512], reads=["wsc_win"], writes=["Wh"], key="win")
                xn = sbt(c2, "xn_a", [P, D], BF16)
                xnT1 = sbt(c2, "xnT1", [P, 8, P], BF16)
                qf = sbt(c2, "qf", [P, 512], F32)
                sgm = sbt(c2, "sgm", [P, 512], F32)
                ff = sbt(c2, "ff", [P, 512], F32)
                kf = sbt(c2, "kf", [P, 512], F32)
                et = [sbt(c2, f"et{i}", [P, 512], F32) for i in range(2)]
                lfh = sbt(c2, "lfh", [P, 512], BF16)
                lfl = sbt(c2, "lfl", [P, 512], BF16)
                qp = sbt(c2, "qp", [P, 512], BF16)
                kp = sbt(c2, "kp", [P, 512], BF16)
                qpp = sbt(c2, "qpp", [P, 512], BF16)
                kpp = sbt(c2, "kpp", [P, 512], BF16)
                vb = sbt(c2, "vb", [P, 512], BF16)
                qkT3 = [sbt(c2, f"qkT{i}", [P, 8, P], BF16) for i in range(3)]
                qpv = [sbt(c2, f"qpv{i}", [P, 512], BF16) for i in range(3)]
                kpv = [sbt(c2, f"kpv{i}", [P, 512], BF16) for i in range(3)]
                ATm = sbt(c2, "ATm", [P, 4, P], BF16)
                bst = [sbt(c2, f"bst{i}", [P, 516], F32) for i in range(2)]
                onec = sbt(c2, "onec", [P, 1], BF16)
                S.op("dve", lambda: V.memset(onec[:], 1.0), writes=["onec"])
                UA, UB, C1, C3, C4, PA = PS[0], PS[1], PS[2], PS[3], PS[4], PS[5]

                def proj(dst, dname, c0):
                    for kc in range(8):
                        S.op("pe", lambda kc=kc: T.matmul(dst[:], lhsT=xnT1[:, kc, :], rhs=Wh[:, kc, c0:c0 + 512],
                                                          start=(kc == 0), stop=(kc == 7)),
                             reads=["xnT1", "Wh"], writes=[dname], sig=(kc == 7))

                for b in range(1, NBL if "p2a" not in skip else 0):
                    rmsnorm_T(H[:, b, :], [f"H{b}", "H"], xn, "xn_a", xnT1[:], "xnT1")
                    proj(UA, "ps0", 0)
                    proj(UB, "ps1", 512)
                    S.op("act", lambda: A.activation(out=qf[:], in_=UA[:], func=AF.Silu), reads=["ps0"], writes=["qf"])
                    S.op("act", lambda: A.activation(out=sgm[:], in_=UB[:], func=AF.Sigmoid), reads=["ps1"], writes=["sgm"])
                    proj(UA, "ps0", 1024)
                    S.op("dve", lambda: V.tensor_tensor(out=ff[:], in0=sgm[:], in1=omlB[:], op=ALU.mult), reads=["sgm", "omlB"], writes=["ff"])
                    S.op("dve", lambda: V.tensor_tensor(out=ff[:], in0=ff[:], in1=lbB[:], op=ALU.add), reads=["ff", "lbB"], writes=["ff"])
                    S.op("act", lambda: A.activation(out=ff[:], in_=ff[:], func=AF.Ln), reads=["ff"], writes=["ff"])
                    S.op("dve", lambda: V.tensor_scalar(out=kf[:], in0=sgm[:], scalar1=-1.0, scalar2=1.0, op0=ALU.mult, op1=ALU.add),
                         reads=["sgm"], writes=["kf"])
                    S.op("dve", lambda: V.tensor_tensor(out=kf[:], in0=kf[:], in1=omlB[:], op=ALU.mult), reads=["kf", "omlB"], writes=["kf"])
                    if b == 0:
                        S.op("dve", lambda: V.tensor_scalar_mul(out=ff[:], in0=ff[:], scalar1=pv[:, 0:1]), reads=["ff", "pv"], writes=["ff"])
                        S.op("dve", lambda: V.tensor_scalar_mul(out=kf[:], in0=kf[:], scalar1=pv[:, 0:1]), reads=["kf", "pv"], writes=["kf"])
                    S.op("dve", lambda: V.tensor_copy(out=lfh[:], in_=ff[:]), reads=["ff"], writes=["lfh"])
                    S.op("dve", lambda: V.tensor_tensor(out=lfl[:], in0=ff[:], in1=lfh[:], op=ALU.subtract), reads=["ff", "lfh"], writes=["lfl"])
                    for (dst, dn, Mx) in ((C1, "ps2", D1), (PS[5], "ps5", D1A), (PS[6], "ps6", D1B), (C3, "ps3", TRI)):
                        S.op("pe", lambda dst=dst, Mx=Mx: T.matmul(dst[:], lhsT=Mx, rhs=lfh[:], start=True, stop=False),
                             reads=["cm", "lfh"], writes=[dn], sig=False)
                        S.op("pe", lambda dst=dst, Mx=Mx: T.matmul(dst[:], lhsT=Mx, rhs=lfl[:], start=False, stop=True),
                             reads=["cm", "lfl"], writes=[dn], sig=True)
                    mA, mB = pv[:, 11:12], pv[:, 12:13]
                    for vi, (src, sn, qbias, kbias) in enumerate(((PS[5], "ps5", mA, mA), (PS[6], "ps6", mB, mB), (C1, "ps2", mB, mA))):
                        S.op("act", lambda src=src, qbias=qbias: A.activation(out=et[0][:], in_=src[:], func=AF.Exp, bias=qbias),
                             reads=[sn, "pv"], writes=["et0"])
                        S.op("dve", lambda vi=vi: V.tensor_tensor(out=qpv[vi][:], in0=qf[:], in1=et[0][:], op=ALU.mult),
                             reads=["qf", "et0"], writes=[f"qpv{vi}"])
                        S.op("act", lambda src=src, kbias=kbias: A.activation(out=et[1][:], in_=src[:], func=AF.Exp, scale=-1.0, bias=kbias),
                             reads=[sn, "pv"], writes=["et1"])
                        S.op("dve", lambda vi=vi: V.tensor_tensor(out=kpv[vi][:], in0=kf[:], in1=et[1][:], op=ALU.mult),
                             reads=["kf", "et1"], writes=[f"kpv{vi}"])
                    S.op("act", lambda: A.activation(out=et[0][:], in_=C3[:], func=AF.Exp), reads=["ps3"], writes=["et0"])
                    S.op("dve", lambda: V.tensor_tensor(out=qpp[:], in0=qf[:], in1=et[0][:], op=ALU.mult), reads=["qf", "et0"], writes=["qpp"])
                    S.op("act", lambda: A.copy(out=vb[:], in_=UA[:]), reads=["ps0"], writes=["vb"])
                    if b >= 1:
                        for vi in range(3):
                            for h in range(4):
                                S.op("pe", lambda h=h, vi=vi: T.transpose(out=TPb[:, h, :], in_=qpv[vi][:, h * P:(h + 1) * P], identity=IDENT),
                                     reads=[f"qpv{vi}", "cm"], writes=["TPb"], sig=False)
                            for h in range(4):
                                S.op("pe", lambda h=h, vi=vi: T.transpose(out=TPb[:, 4 + h, :], in_=kpv[vi][:, h * P:(h + 1) * P], identity=IDENT),
                                     reads=[f"kpv{vi}", "cm"], writes=["TPb"], sig=(h == 3))
                            S.op("act", lambda vi=vi: A.copy(out=qkT3[vi][:], in_=TPb[:]), reads=["TPb"], writes=[f"qkT{vi}"])
                        AT3 = C1[:].rearrange("p (h t) -> p h t", h=4)
                        for h in range(4):
                            for vi in range(3):
                                S.op("pe", lambda h=h, vi=vi: T.matmul(C1[:, h * P:(h + 1) * P], lhsT=qkT3[vi][:, 4 + h, :], rhs=qkT3[vi][:, h, :],
                                                                       start=(vi == 0), stop=(vi == 2)),
                                     reads=[f"qkT{vi}"], writes=["ps2"], sig=(h == 3 and vi == 2))
                        S.op("dve", lambda: V.tensor_tensor(out=ATm[:], in0=AT3, in1=TRI.unsqueeze(1).to_broadcast([P, 4, P]), op=ALU.mult),
                             reads=["ps2", "cm"], writes=["ATm"])
                        for h in range(4):
                            S.op("pe", lambda h=h: T.matmul(C3[:, h * P:(h + 1) * P], lhsT=ATm[:, h, :], rhs=vb[:, h * P:(h + 1) * P], start=True, stop=True),
                                 reads=["ATm", "vb"], writes=["ps3"], sig=(h == 3))
                        S.op("act", lambda: A.copy(out=oin[:, b - 1, :], in_=C3[:]), reads=["ps3"], writes=["oin"])
                        for h in range(4):
                            S.op("pe", lambda h=h: T.transpose(out=TPb[:, h, :], in_=qpp[:, h * P:(h + 1) * P], identity=IDENT),
                                 reads=["qpp", "cm"], writes=["TPb"], sig=(h == 3))
                        S.op("act", lambda: A.copy(out=qppT[:, :, (b - 1) * P:b * P], in_=TPb[:, 0:4, :]), reads=["TPb"], writes=["qppT"])
                S.barrier()
            if dbg.get("_stop") == 21:
                dbg_dump("oin", oin[:], ["oin"])
                dbg_dump("qppT", qppT[:], ["qppT"])
                S.barrier()
                S.final_wait(["dbg"])
                return nc

            with ExitStack() as c2:
                Wq = sbt(c2, "Wq", [P, 8, 512], BF16)
                S.dma("sp", Wq[:], win_v[:, :, 2048:2560], reads=["wsc_win"], writes=["Wq"], key="win")
                QS = []
                for k in range(2):
                    QS.append(dict(xn=sbt(c2, f"xn_b{k}", [P, D], BF16), xnT1=sbt(c2, f"xnT1b{k}", [P, 8, P], BF16),
                                   t32=sbt(c2, f"t32b{k}", [P, 512], F32), nrm=sbt(c2, f"nrmb{k}", [P, 512], BF16),
                                   s8=sbt(c2, f"s8b{k}", [P, 8], F32)))

                def p2b(b, k):
                    d = QS[k]
                    xn, xnT1, t32, nrm, s8 = d["xn"], d["xnT1"], d["t32"], d["nrm"], d["s8"]
                    UQ, uq = PS[k], f"ps{k}"
                    yield from rmsnorm_gen(H[:, b, :], [f"H{b}", "H"], xn, f"xn_b{k}", xnT1[:], f"xnT1b{k}")
                    for kc in range(8):
                        S.op("pe", lambda kc=kc: T.matmul(UQ[:], lhsT=xnT1[:, kc, :], rhs=Wq[:, kc, :], start=(kc == 0), stop=(kc == 7)),
                             reads=[f"xnT1b{k}", "Wq"], writes=[uq], sig=(kc == 7))
                    yield
                    yield S.op("act", lambda: A.activation(out=t32[:], in_=UQ[:], func=AF.Square), reads=[uq], writes=[f"t32b{k}"])
                    yield S.op("dve", lambda: V.tensor_reduce(out=s8[:], in_=t32[:].rearrange("p (h d) -> p h d", h=8), axis=AX.X, op=ALU.add),
                               reads=[f"t32b{k}"], writes=[f"s8b{k}"])
                    yield S.op("act", lambda: A.activation(out=s8[:], in_=s8[:], func=AF.Ln, scale=1.0 / 64, bias=EPS), reads=[f"s8b{k}"], writes=[f"s8b{k}"])
                    yield S.op("act", lambda: A.activation(out=s8[:], in_=s8[:], func=AF.Exp, scale=-0.5), reads=[f"s8b{k}"], writes=[f"s8b{k}"])
                    yield S.op("dve", lambda: V.tensor_tensor(out=t32[:].rearrange("p (h d) -> p h d", h=8),
                                                              in0=UQ[:].rearrange("p (h d) -> p h d", h=8),
                                                              in1=s8[:].unsqueeze(2).to_broadcast([P, 8, 64]), op=ALU.mult),
                               reads=[uq, f"s8b{k}"], writes=[f"t32b{k}"])
                    yield S.op("dve", lambda: V.tensor_tensor(out=nrm[:].rearrange("p (h d) -> p h d", h=8),
                                                              in0=t32[:].rearrange("p (h d) -> p h d", h=8),
                                                              in1=qgB[:].unsqueeze(1).to_broadcast([P, 8, 64]), op=ALU.mult),
                               reads=[f"t32b{k}", "qgB"], writes=[f"nrmb{k}"])
                    for hp in range(4):
                        S.op("pe", lambda hp=hp: T.transpose(out=TPb[:, hp, :], in_=nrm[:, hp * P:(hp + 1) * P], identity=IDENT),
                             reads=[f"nrmb{k}", "cm"], writes=["TPb"], sig=(hp == 3))
                    yield S.op("act", lambda: A.copy(out=QT[:, :, (b - 1) * P:b * P], in_=TPb[:, 0:4, :]), reads=["TPb"], writes=["QT"])

                if "p2b" not in skip:
                    for b0 in range(1, NBL, 2):
                        lockstep([p2b(b, b - b0) for b in range(b0, min(b0 + 2, NBL))])
                S.barrier()
            if dbg.get("_stop") == 22:
                dbg_dump("QT", QT[:], ["QT"])
                S.barrier()
                S.final_wait(["dbg"])
                return nc
            S.barrier()
            with ExitStack() as c2:
                Wg = sbt(c2, "Wg", [P, 8, 512], BF16)
                Wo1 = sbt(c2, "Wo1", [P, 4, D], BF16)
                S.dma("sp", Wg[:], win_v[:, :, 1536:2048], reads=["wsc_win"], writes=["Wg"], key="win")
                S.dma("sp", Wo1[:], wsc["wout"][0:512, :].rearrange("(h p) n -> p h n", p=P), reads=["wsc_wout"], writes=["Wo1"], key="wo")
                Sst = sbt(c2, "Sst", [P, 512], F32)
                Ssel = [sbt(c2, f"Ssel{i}", [P, 512], F32) for i in range(2)]
                Sb = sbt(c2, "Sb", [P, 512], BF16)
                Bl = [sbt(c2, f"Bl{i}", [P, 516], F32) for i in range(3)]
                xn = sbt(c2, "xn_c", [P, D], BF16)
                xnT1 = sbt(c2, "xnT1c", [P, 8, P], BF16)
                sgl = sbt(c2, "sgl", [P, 512], F32)
                ot = sbt(c2, "ot", [P, 512], F32)
                t32 = sbt(c2, "t32c", [P, 512], F32)
                ob = sbt(c2, "ob", [P, 512], BF16)
                obT = sbt(c2, "obT", [P, 4, P], BF16)
                s4 = sbt(c2, "s4", [P, 4], F32)
                S.op("dve", lambda: V.memset(Sst[:], 0.0), writes=["Sst"])
                UA, OI = PS[0], PS[1]

                def block_out(l, sel, seln):
                    S.op("dve", lambda: V.tensor_copy(out=Sb[:], in_=sel[:]), reads=[seln], writes=["Sb"])
                    rmsnorm_T(H[:, l, :], [f"H{l}", "H"], xn, "xn_c", xnT1[:], "xnT1c")
                    for kc in range(8):
                        S.op("pe", lambda kc=kc: T.matmul(UA[:], lhsT=xnT1[:, kc, :], rhs=Wg[:, kc, :], start=(kc == 0), stop=(kc == 7)),
                             reads=["xnT1c", "Wg"], writes=["ps0"], sig=(kc == 7))
                    S.op("act", lambda: A.activation(out=sgl[:], in_=UA[:], func=AF.Silu), reads=["ps0"], writes=["sgl"])
                    for h in range(4):
                        S.op("pe", lambda h=h: T.matmul(OI[:, h * P:(h + 1) * P], lhsT=qppT[:, h, (l - 1) * P:l * P], rhs=Sb[:, h * P:(h + 1) * P],
                                                        start=True, stop=True),
                             reads=["qppT", "Sb"], writes=["ps1"], sig=(h == 3))
                    S.op("dve", lambda: V.tensor_tensor(out=ot[:], in0=OI[:], in1=oin[:, l - 1, :], op=ALU.add), reads=["ps1", "oin"], writes=["ot"])
                    S.op("act", lambda: A.activation(out=t32[:], in_=ot[:], func=AF.Square), reads=["ot"], writes=["t32c"])
                    S.op("dve", lambda: V.tensor_reduce(out=s4[:], in_=t32[:].rearrange("p (h d) -> p h d", h=4), axis=AX.X, op=ALU.add),
                         reads=["t32c"], writes=["s4"])
                    S.op("act", lambda: A.activation(out=s4[:], in_=s4[:], func=AF.Ln, scale=1.0 / 128, bias=EPS), reads=["s4"], writes=["s4"])
                    S.op("act", lambda: A.activation(out=s4[:], in_=s4[:], func=AF.Exp, scale=-0.5), reads=["s4"], writes=["s4"])
                    S.op("dve", lambda: V.tensor_tensor(out=ot[:].rearrange("p (h d) -> p h d", h=4), in0=ot[:].rearrange("p (h d) -> p h d", h=4),
                                                        in1=s4[:].unsqueeze(2).to_broadcast([P, 4, P]), op=ALU.mult),
                         reads=["ot", "s4"], writes=["ot"])
                    S.op("dve", lambda: V.tensor_tensor(out=ot[:], in0=ot[:], in1=ognB[:], op=ALU.mult), reads=["ot", "ognB"], writes=["ot"])
                    S.op("dve", lambda: V.tensor_tensor(out=ob[:], in0=ot[:], in1=sgl[:], op=ALU.mult), reads=["ot", "sgl"], writes=["ob"])
                    for h in range(4):
                        S.op("pe", lambda h=h: T.transpose(out=TPb[:, h, :], in_=ob[:, h * P:(h + 1) * P], identity=IDENT),
                             reads=["ob", "cm"], writes=["TPb"], sig=(h == 3))
                    S.op("act", lambda: A.copy(out=obT[:], in_=TPb[:, 0:4, :]), reads=["TPb"], writes=["obT"])
                    for half in range(2):
                        ys = 4 + half
                        for h in range(4):
                            S.op("pe", lambda h=h: T.matmul(PS[ys][:], lhsT=obT[:, h, :], rhs=Wo1[:, h, half * 512:(half + 1) * 512],
                                                            start=(h == 0), stop=(h == 3)),
                                 reads=["obT", "Wo1"], writes=[f"ps{ys}"], sig=(h == 3))
                        hs = H[:, l, half * 512:(half + 1) * 512]
                        S.op("dve", lambda: V.tensor_tensor(out=hs, in0=PS[ys][:], in1=hs, op=ALU.add),
                             reads=[f"ps{ys}", f"H{l}", "H"], writes=[f"H{l}"])

                def rl_of(g):
                    return (0, 0) if g == 0 else ((g - 1) % 8, 1 + (g - 1) // 8)

                def load_B(g):
                    r_, l_ = rl_of(g)
                    S.dma("sp", Bl[g % 3][:].bitcast(BF16), comb_v[g][:, 0:1032], reads=["comb_g"], writes=[f"Bl{g % 3}"], key=f"bl{g % 3}")

                load_B(0)
                load_B(1)
                for g in range(129):
                    r, l = rl_of(g)
                    sl = g % 3
                    sel = Ssel[l % 2]
                    seln = f"Ssel{l % 2}"
                    if g >= 1:
                        if r == 0:
                            S.op("dve", lambda: V.tensor_scalar_mul(out=sel[:], in0=Sst[:], scalar1=pv[:, 2:3]), reads=["Sst", "pv"], writes=[seln])
                        else:
                            S.op("dve", lambda: V.scalar_tensor_tensor(out=sel[:], in0=Sst[:], scalar=pv[:, 2 + r:3 + r], in1=sel[:],
                                                                       op0=ALU.mult, op1=ALU.add),
                                 reads=["Sst", "pv", seln], writes=[seln])
                    if g + 2 < 128:
                        load_B(g + 2)
                    if g < 128:
                        S.op("dve", lambda: V.tensor_tensor(out=Sst[:].rearrange("p (h d) -> p h d", h=4), in0=Sst[:].rearrange("p (h d) -> p h d", h=4),
                                                            in1=Bl[sl][:, 512:516].unsqueeze(2).to_broadcast([P, 4, P]), op=ALU.mult),
                             reads=["Sst", f"Bl{sl}"], writes=["Sst"])
                        S.op("dve", lambda: V.tensor_tensor(out=Sst[:], in0=Sst[:], in1=Bl[sl][:, 0:512], op=ALU.add),
                             reads=["Sst", f"Bl{sl}"], writes=["Sst"])
                    if g >= 1 and r == 7:
                        block_out(l, sel, seln)
                S.barrier()
            if "h2a" in dbg_d:
                dbg_dump("h2a", H[:], ["H"] + [f"H{b}" for b in range(NBL)])
            if dbg.get("_stop") == 2:
                S.barrier()
                S.final_wait(["dbg"])
                return nc

            S.barrier()
            cA.close()
            pstk.close()
            pstk = ExitStack()
            Zp = [pstk.enter_context(nc.psum_tensor(f"psz{i}", [P, 2, 512], F32)) for i in range(4)]
            Ap = Zp
            with ExitStack() as c2:
                osbT = sbt(c2, "osbT", [P, 4, NQ * P], BF16)
                c3 = ExitStack()
                pm = sbt(c3, "pm", [P, 16, P], BF16)
                S.dma("sp", pm[:], pmask_d.rearrange("p (k c) -> p k c", k=16), writes=["pm"], key="init")
                OutAcc = [sbt(c3, f"OutAcc{i}", [P, NQ * P], F32) for i in range(2)]
                Lsum = [sbt(c3, f"Lsum{i}", [P, 2, NQ * P], BF16) for i in range(2)]
                Ks = [[sbt(c3, f"Ks{si}_{i}", [P, 8, P], BF16) for i in range(2)] for si in range(2)]
                Vs = [[sbt(c3, f"Vs{si}_{i}", [P, 8, P], BF16) for i in range(2)] for si in range(2)]
                NSL = 4
                e_t = [sbt(c3, f"e_t{i}", [P, 2, 512], F32) for i in range(NSL)]
                lk_t = [sbt(c3, f"lk_t{i}", [P, 2, 512], BF16) for i in range(NSL)]
                w_t = [sbt(c3, f"w_t{i}", [P, 2, 512], BF16) for i in range(NSL)]
                kvg_v = comb_g.rearrange("(g p) c -> p g c", p=P)[:, :, 1032:2056]
                PR = [slice(0, 64), slice(64, 128)]
                gcount = [0, 0]
                for hp0 in (0, 2):
                    ulist = [[], []]
                    groups = [[], []]
                    for si in range(2):
                        S.op("dve", lambda: V.memset(OutAcc[si][:], 0.0), writes=[f"OA{si}_{q}" for q in range(4)])
                        S.op("dve", lambda: V.memset(Lsum[si][:], 0.0), writes=[f"LS{si}_{q}" for q in range(4)])
                        for Gi in list(range(15, -1, -1)) + [-1]:
                            sl = gcount[si] % 2
                            gcount[si] += 1
                            rs = list(range(7, -1, -1)) if Gi >= 0 else [0]
                            first = len(ulist[si])
                            for r in rs:
                                for q in range(4):
                                    j0 = max(Gi, 4 * q) if Gi >= 0 else 4 * q
                                    if j0 > 4 * q + 3:
                                        continue
                                    ulist[si].append(dict(si=si, hp=hp0 + si, sl=sl, r=r, q=q, j0=j0, j1=4 * q + 4, meta=(Gi < 0),
                                                          bnd=(Gi >= 0 and j0 == Gi)))
                            groups[si].append((Gi, sl, 2 * first + si, 2 * (len(ulist[si]) - 1) + si))
                    units = [u for pair in zip(ulist[0], ulist[1]) for u in pair]

                    def load_group(si, gi):
                        Gi, sl, _, _ = groups[si][gi]
                        hp = hp0 + si
                        kn = f"Ks{si}_{sl}"
                        if Gi >= 0:
                            S.dma("sp", Ks[si][sl][:], kvg_v[:, 8 * Gi + 1:8 * Gi + 9, hp * P:(hp + 1) * P], reads=["comb_g"], writes=[kn], key=kn)
                            S.dma("sp", Vs[si][sl][:], kvg_v[:, 8 * Gi + 1:8 * Gi + 9, (4 + hp) * P:(5 + hp) * P], reads=["comb_g"], writes=[kn + "v"], key=kn)
                        else:
                            S.dma("sp", Ks[si][sl][:, 0, :], kvg_v[:, 0, hp * P:(hp + 1) * P], reads=["comb_g"], writes=[kn], key=kn)
                            S.dma("sp", Vs[si][sl][:, 0, :], kvg_v[:, 0, (4 + hp) * P:(5 + hp) * P], reads=["comb_g"], writes=[kn + "v"], key=kn)

                    def st1(u, i):
                        s = i % NSL
                        si, hp, sl, r = u["si"], u["hp"], u["sl"], u["r"]
                        c0, c1 = u["j0"] * P, u["j1"] * P
                        n = c1 - c0
                        u["n"], u["c0"], u["c1"], u["s"] = n, c0, c1, s
                        kn = f"Ks{si}_{sl}"
                        for hh in range(2):
                            pr = PR[hh]
                            S.op("pe", lambda: T.matmul(Zp[s][:, hh, 0:n], lhsT=Ks[si][sl][pr, r, :], rhs=QT[pr, hp, c0:c1], start=True, stop=False,
                                                        skip_group_check=True),
                                 reads=[kn, "QT"], writes=[f"psz{s}"], sig=(hh == 1))
                        S.op("act", lambda: A.activation(out=e_t[s][:, :, 0:n], in_=Zp[s][:, :, 0:n], func=AF.Exp), reads=[f"psz{s}"], writes=[f"e_t{s}"])
                        S.op("act", lambda: A.activation(out=lk_t[s][:, :, 0:n], in_=e_t[s][:, :, 0:n], func=AF.Ln, bias=1.0),
                             reads=[f"e_t{s}"], writes=[f"lk_t{s}"])
                        if u["bnd"]:
                            S.op("dve", lambda: V.tensor_tensor(out=lk_t[s][:, :, 0:P], in0=lk_t[s][:, :, 0:P],
                                                                in1=pm[:, r, :].unsqueeze(1).to_broadcast([P, 2, P]), op=ALU.mult),
                                 reads=[f"lk_t{s}", "pm"], writes=[f"lk_t{s}"])
                        if u["meta"]:
                            S.op("dve", lambda: V.tensor_scalar_mul(out=lk_t[s][:, :, 0:n], in0=lk_t[s][:, :, 0:n], scalar1=pv[:, 0:1]),
                                 reads=[f"lk_t{s}", "pv"], writes=[f"lk_t{s}"])

                    def st2(u):
                        s, n, c0, c1, sl, r, q, si = u["s"], u["n"], u["c0"], u["c1"], u["sl"], u["r"], u["q"], u["si"]
                        lsn = f"LS{si}_{q}"
                        for hh in range(2):
                            S.op("pe", lambda: T.matmul(Zp[s][:, hh, 0:n], lhsT=NUI, rhs=lk_t[s][:, hh, 0:n], start=False, stop=False,
                                                        skip_group_check=True),
                                 reads=["cm", f"lk_t{s}"], writes=[f"psz{s}"], sig=False)
                            S.op("pe", lambda: T.matmul(Zp[s][:, hh, 0:n], lhsT=NONES, rhs=Lsum[si][:, hh, c0:c1], start=False, stop=(not u["bnd"]),
                                                        skip_group_check=True),
                                 reads=["cm", lsn], writes=[f"psz{s}"], sig=(hh == 1 and not u["bnd"]))
                            if u["bnd"]:
                                S.op("pe", lambda: T.matmul(Zp[s][:, hh, 0:P], lhsT=IDENT, rhs=pm[:, 8 + r, :], start=False, stop=True,
                                                            skip_group_check=True),
                                     reads=["cm", "pm"], writes=[f"psz{s}"], sig=(hh == 1))
                        S.op("dve", lambda: V.tensor_tensor(out=Lsum[si][:, :, c0:c1], in0=Lsum[si][:, :, c0:c1], in1=lk_t[s][:, :, 0:n], op=ALU.add),
                             reads=[lsn, f"lk_t{s}"], writes=[lsn])
                        if u["meta"]:
                            S.op("act", lambda: A.activation(out=w_t[s][:, :, 0:n], in_=Zp[s][:, :, 0:n], func=AF.Exp, bias=pv[:, 1:2]),
                                 reads=[f"psz{s}", "pv"], writes=[f"w_t{s}"])
                        else:
                            S.op("act", lambda: A.activation(out=w_t[s][:, :, 0:n], in_=Zp[s][:, :, 0:n], func=AF.Exp),
                                 reads=[f"psz{s}"], writes=[f"w_t{s}"])

                    def st3(u):
                        s, n, c0, c1, sl, r, q, si = u["s"], u["n"], u["c0"], u["c1"], u["sl"], u["r"], u["q"], u["si"]
                        oan = f"OA{si}_{q}"
                        for hh in range(2):
                            S.op("pe", lambda: T.matmul(Zp[s][:, hh, 0:n], lhsT=Vs[si][sl][:, r, :], rhs=w_t[s][:, hh, 0:n], start=True, stop=True),
                                 reads=[f"Ks{si}_{sl}v", f"w_t{s}"], writes=[f"psz{s}"], sig=(hh == 1))
                        for hh in range(2):
                            pr = PR[hh]
                            S.op("dve", lambda: V.tensor_tensor(out=OutAcc[si][pr, c0:c1], in0=OutAcc[si][pr, c0:c1], in1=Zp[s][pr, hh, 0:n], op=ALU.add),
                                 reads=[oan, f"psz{s}"], writes=[oan])

                    nu = len(units)
                    nxt = [2, 2]
                    for si in range(2):
                        load_group(si, 0)
                        load_group(si, 1)
                    for i in range(nu + 2):
                        if i < nu:
                            st1(units[i], i)
                        if 0 <= i - 1 < nu:
                            st2(units[i - 1])
                        if 0 <= i - 2 < nu:
                            st3(units[i - 2])
                        for si in range(2):
                            if nxt[si] < len(groups[si]) and i - 2 >= groups[si][nxt[si] - 2][3]:
                                load_group(si, nxt[si])
                                nxt[si] += 1
                    assert nxt == [len(groups[0]), len(groups[1])]
                    for si in range(2):
                        S.op("act", lambda: A.copy(out=osbT[:, hp0 + si, :], in_=OutAcc[si][:]), reads=[f"OA{si}_{q}" for q in range(4)], writes=["osbT"])

                S.barrier()
                c3.close()
                Wo2 = sbt(c2, "Wo2", [P, 4, D], BF16)
                S.dma("sp", Wo2[:], wsc["wout"][512:1024, :].rearrange("(h p) n -> p h n", p=P), reads=["wsc_wout"], writes=["Wo2"], key="wo")
                yc = 0
                for l in range(1, NBL):
                    for half in range(2):
                        ys = yc % 2
                        yc += 1
                        yb = Zp[ys][:, 0, :]
                        for hp in range(4):
                            S.op("pe", lambda hp=hp: T.matmul(yb, lhsT=osbT[:, hp, (l - 1) * P:l * P], rhs=Wo2[:, hp, half * 512:(half + 1) * 512],
                                                              start=(hp == 0), stop=(hp == 3)),
                                 reads=["osbT", "Wo2"], writes=[f"psz{ys}"], sig=(hp == 3))
                        hs = H[:, l, half * 512:(half + 1) * 512]
                        S.op("dve", lambda: V.tensor_tensor(out=hs, in0=yb, in1=hs, op=ALU.add),
                             reads=[f"psz{ys}", f"H{l}", "H"], writes=[f"H{l}"])
                S.barrier()
            pstk.close()
            pstk = ExitStack()
            PS = [pstk.enter_context(nc.psum_tensor(f"ps{i}c", [P, 512], F32)) for i in range(7)]
            TPb = pstk.enter_context(nc.psum_tensor("tpbc", [P, 8, P], BF16))
        if "h2" in dbg_d:
            dbg_dump("h2", H[:], ["H"] + [f"H{b}" for b in range(NBL)])
        if dbg.get("_stop") == 3:
            S.barrier()
            S.final_wait(["dbg"])
            return nc

        ffn([list(range(1, 9)), list(range(9, 17))], f2n_d, "f2wi", "f2wo", out_d=y_d)
        S.barrier()
        S.final_wait(["out"] + (["dbg"] if dbg_d else []))
        pstk.close()
    return nc


def _const_mats():
    s = np.arange(P)[:, None]
    t = np.arange(P)[None, :]
    ident = (s == t).astype(np.float32)
    tri = (s <= t).astype(np.float32)
    sel63 = np.broadcast_to((s <= 63), (P, P)).astype(np.float32)
    d1 = tri - sel63
    d4 = 1.0 - tri
    nui = -(s >= t).astype(np.float32)
    nones = -np.ones((P, P), np.float32)
    d1a = tri - np.broadcast_to((s <= 31), (P, P)).astype(np.float32)
    d1b = tri - np.broadcast_to((s <= 95), (P, P)).astype(np.float32)
    return np.concatenate([ident, tri, d1, d4, nui, nones, d1a, d1b], axis=1).astype(ml_dtypes.bfloat16)


def _core_masks(c):
    s = np.arange(P)[:, None]
    t = np.arange(P)[None, :]
    m01 = []
    for r in range(8):
        if r < c:
            m = np.ones((P, P), np.float32)
        elif r == c:
            m = (s < t).astype(np.float32)
        else:
            m = np.zeros((P, P), np.float32)
        m01.append(m)
    negm = [NEG * (1.0 - m) for m in m01]
    pmask = np.concatenate(m01 + negm, axis=1).astype(ml_dtypes.bfloat16)
    pvec = np.zeros((P, 16), np.float32)
    pvec[:, 0] = (np.arange(P) >= 112).astype(np.float32)
    pvec[:, 1] = NEG * (np.arange(P) < 112).astype(np.float32)
    for r in range(8):
        pvec[:, 2 + r] = 1.0 if r == c else 0.0
    pvec[:, 10] = 1.0
    pvec[:, 11] = NEG * (np.arange(P) >= 64)
    pvec[:, 12] = NEG * (np.arange(P) < 64)
    return pmask, pvec


def make_in_maps(inputs):
    f32 = lambda a: np.ascontiguousarray(np.asarray(a, dtype=np.float32))
    x = f32(inputs["x"])[0]
    meta = f32(inputs["meta_tokens"])
    blk0 = np.concatenate([np.zeros((P - 16, D), np.float32), meta], axis=0)
    xb = x.reshape(128, P, D)
    cmat = _const_mats()
    shared = {
        "ffn1_norm": f32(inputs["ffn1_norm"]).reshape(1, D),
        "ffn1_w_in": f32(inputs["ffn1_w_in"])[0],
        "ffn1_w_out": f32(inputs["ffn1_w_out"])[0],
        "mix_norm": f32(inputs["mix_norm"]).reshape(1, D),
        "w_in": f32(inputs["w_in"])[0],
        "hgrn_lb_logits": f32(inputs["hgrn_lb_logits"]).reshape(1, 1024),
        "hgrn_out_norm": f32(inputs["hgrn_out_norm"]).reshape(1, 512),
        "sb_q_norm": f32(inputs["sb_q_norm"]).reshape(1, 64),
        "sb_k_norm": f32(inputs["sb_k_norm"]).reshape(1, 64),
        "w_out": f32(inputs["w_out"])[0],
        "ffn2_norm": f32(inputs["ffn2_norm"]).reshape(1, D),
        "ffn2_w_in": f32(inputs["ffn2_w_in"])[0],
        "ffn2_w_out": f32(inputs["ffn2_w_out"])[0],
        "cmat": cmat,
    }
    xall = np.ascontiguousarray(np.concatenate([blk0[None], xb], axis=0))
    shared["xall"] = xall
    in_maps = []
    for c in range(NCORES):
        xc = np.concatenate([blk0[None], xb[c::8]], axis=0)
        pmask, pvec = _core_masks(c)
        m = dict(shared)
        m["x"] = np.ascontiguousarray(xc)
        m["pmask"] = pmask
        m["pvec"] = pvec
        in_maps.append(m)
    return in_maps


_NC_CACHE = {}


def kernel(**inputs):
    if "nc" not in _NC_CACHE:
        _NC_CACHE["nc"] = build_program()
    nc = _NC_CACHE["nc"]
    in_maps = make_in_maps(inputs)
    res = run_bass_kernel_spmd(nc, in_maps, core_ids=list(range(NCORES)))
    out = np.zeros((128, P, D), np.float32)
    for c in range(NCORES):
        out[c::8] = res.results[c]["y"]
    return out.reshape(1, 128 * P, D)
```
